# Optimizing a Trainium2 kernel written in Bass

```python
import math
import jax, jax.numpy as jnp
from jax import lax
import numpy as np

D_MODEL = 1024
BATCH = 16
SEQ = 2048
DEPTH = 1

HEAD_DIM = 64
N_HEADS_FOX = 8
N_HEADS_SB = 8
D_FOX = N_HEADS_FOX * HEAD_DIM
D_SB = N_HEADS_SB * HEAD_DIM
D_FF = 4 * D_MODEL
D_PLE = 256
Q_BLOCK = 128
EPS = 1e-6
IN_SIZES = (D_FOX, D_FOX, D_FOX, N_HEADS_FOX, D_SB, D_SB, D_SB, D_MODEL, D_MODEL)
D_IN = 3 * D_FOX + N_HEADS_FOX + 3 * D_SB + 2 * D_MODEL

kernel_name = "fox_stickbreaking_gated_hybrid_block"


def _rmsnorm(x, g):
    xf = x.astype(jnp.float32)
    r = lax.rsqrt(jnp.mean(xf * xf, axis=-1, keepdims=True) + EPS)
    return (xf * r * g.astype(jnp.float32)).astype(x.dtype)


def _split_cols(u):
    offs = []
    o = 0
    for s in IN_SIZES[:-1]:
        o += s
        offs.append(o)
    return jnp.split(u, offs, axis=-1)


def _heads(u, n_heads):
    b, s, _ = u.shape
    return u.reshape(b, s, n_heads, HEAD_DIM).transpose(0, 2, 1, 3).astype(jnp.float32)


def _merge_heads(o, dtype):
    b, h, s, d = o.shape
    return o.transpose(0, 2, 1, 3).reshape(b, s, h * d).astype(dtype)


def _forgetting_attention(q, k, v, log_f):
    s_len = q.shape[2]
    scale = HEAD_DIM ** -0.5
    c = jnp.cumsum(log_f, axis=-1)
    outs = []
    for blk in range(s_len // Q_BLOCK):
        q0 = blk * Q_BLOCK
        k_end = q0 + Q_BLOCK
        qb = q[:, :, q0:k_end]
        kb = k[:, :, :k_end]
        vb = v[:, :, :k_end]
        logits = (jnp.einsum('bhqd,bhkd->bhqk', qb, kb) * scale
                  + c[:, :, q0:k_end, None] - c[:, :, None, :k_end])
        q_pos = q0 + jnp.arange(Q_BLOCK)
        k_pos = jnp.arange(k_end)
        causal = k_pos[None, :] <= q_pos[:, None]
        logits = jnp.where(causal, logits, -jnp.inf)
        probs = jax.nn.softmax(logits, axis=-1)
        outs.append(jnp.einsum('bhqk,bhkd->bhqd', probs, vb))
    return jnp.concatenate(outs, axis=2)


def _stick_breaking_attention(q, k, v):
    s_len = q.shape[2]
    scale = HEAD_DIM ** -0.5
    outs = []
    for blk in range(s_len // Q_BLOCK):
        q0 = blk * Q_BLOCK
        k_end = q0 + Q_BLOCK
        qb = q[:, :, q0:k_end]
        kb = k[:, :, :k_end]
        vb = v[:, :, :k_end]
        z = jnp.einsum('bhqd,bhkd->bhqk', qb, kb) * scale
        q_pos = q0 + jnp.arange(Q_BLOCK)
        k_pos = jnp.arange(k_end)
        strict = k_pos[None, :] < q_pos[:, None]
        log_1m_beta = jnp.where(strict, jax.nn.log_sigmoid(-z), 0.0)
        tail = lax.cumsum(log_1m_beta, axis=3, reverse=True) - log_1m_beta
        weights = jnp.where(strict, jnp.exp(jax.nn.log_sigmoid(z) + tail), 0.0)
        outs.append(jnp.einsum('bhqk,bhkd->bhqd', weights, vb))
    return jnp.concatenate(outs, axis=2)


def setup_inputs(seed: int = 0) -> dict:
    key = jax.random.key(seed)
    ks = jax.random.split(key, 16)
    f32 = jnp.float32

    def w(k, shape, fan_in, gain=1.0):
        return jax.random.normal(k, shape, f32) * (gain * fan_in ** -0.5)

    def gain(k, shape):
        return 1.0 + 0.02 * jax.random.normal(k, shape, f32)

    return {
        "x": jax.random.normal(ks[0], (BATCH, SEQ, D_MODEL), f32),
        "p": jax.random.normal(ks[1], (DEPTH, BATCH, SEQ, D_PLE), f32),
        "g_mix": gain(ks[2], (DEPTH, D_MODEL)),
        "w_in": w(ks[3], (DEPTH, D_MODEL, D_IN), D_MODEL),
        "b_forget": jax.random.uniform(ks[4], (DEPTH, N_HEADS_FOX), f32, 1.0, 4.0),
        "b_gate": 0.01 * jax.random.normal(ks[5], (DEPTH, 2, D_MODEL), f32),
        "w_branch_fox": w(ks[6], (DEPTH, D_FOX, D_MODEL), D_FOX),
        "w_branch_sb": w(ks[7], (DEPTH, D_SB, D_MODEL), D_SB),
        "w_out": w(ks[8], (DEPTH, D_MODEL, D_MODEL), D_MODEL),
        "g_mlp": gain(ks[9], (DEPTH, D_MODEL)),
        "w_up": w(ks[10], (DEPTH, D_MODEL, D_FF), D_MODEL),
        "w_down": w(ks[11], (DEPTH, D_FF, D_MODEL), D_FF, gain=0.5),
        "g_ple": gain(ks[12], (DEPTH, D_MODEL)),
        "w_ple_gate": w(ks[13], (DEPTH, D_MODEL, D_MODEL), D_MODEL),
        "w_ple": w(ks[14], (DEPTH, D_PLE, D_MODEL), D_PLE),
        "g_final": gain(ks[15], (D_MODEL,)),
    }


def reference(x, p, g_mix, w_in, b_forget, b_gate, w_branch_fox, w_branch_sb, w_out,
              g_mlp, w_up, w_down, g_ple, w_ple_gate, w_ple, g_final):
    b, s, _ = x.shape
    for i in range(DEPTH):
        h = _rmsnorm(x, g_mix[i])
        u = h @ w_in[i]
        q_a, k_a, v_a, f_a, q_b, k_b, v_b, gl_a, gl_b = _split_cols(u)
        log_f = jax.nn.log_sigmoid((f_a + b_forget[i]).astype(jnp.float32)).transpose(0, 2, 1)
        o_fox = _forgetting_attention(_heads(q_a, N_HEADS_FOX), _heads(k_a, N_HEADS_FOX),
                                      _heads(v_a, N_HEADS_FOX), log_f)
        o_sb = _stick_breaking_attention(_heads(q_b, N_HEADS_SB), _heads(k_b, N_HEADS_SB),
                                         _heads(v_b, N_HEADS_SB))
        o_fox = _merge_heads(o_fox, x.dtype) @ w_branch_fox[i]
        o_sb = _merge_heads(o_sb, x.dtype) @ w_branch_sb[i]
        merged = (jax.nn.sigmoid(gl_a + b_gate[i, 0]) * o_fox
                  + jax.nn.sigmoid(gl_b + b_gate[i, 1]) * o_sb)
        x = x + merged @ w_out[i]
        h = _rmsnorm(x, g_mlp[i])
        x = x + jnp.square(jax.nn.relu(h @ w_up[i])) @ w_down[i]
        h = _rmsnorm(x, g_ple[i])
        x = x + jax.nn.sigmoid(h @ w_ple_gate[i]) * (p[i] @ w_ple[i])
    return _rmsnorm(x, g_final)
```

```python
from contextlib import ExitStack
import numpy as np
import concourse.bass as bass
import concourse.mybir as mybir
from concourse.bass_utils import run_bass_kernel_spmd
from concourse.ap import AP

F32 = mybir.dt.float32
BF16 = mybir.dt.bfloat16
U8 = mybir.dt.uint8
AF = mybir.ActivationFunctionType
ALU = mybir.AluOpType

NCORES = 8
S = 2048
DM = 1024
NB = 16
NSEQ = 2
EPS = 1e-6
QF, KF, VF, FF, QS, KS, VS, GA, GB = 0, 512, 1024, 1536, 1544, 2056, 2568, 3080, 4104
NEG = -30000.0


class Buf:
    __slots__ = ("name", "last_w", "readers")

    def __init__(self, name):
        self.name = name
        self.last_w = None
        self.readers = []


class Op:
    __slots__ = ("eng", "fn", "deps", "signal", "semkey", "semval", "dma", "idx")


class Prog:
    ENGS = ("pe", "act", "dve", "pool", "sp")

    def __init__(self):
        self.ops = {e: [] for e in self.ENGS}
        self.n = 0

    def add(self, eng, fn, reads=(), writes=(), dma=False, semkey=None):
        op = Op()
        op.eng, op.fn, op.dma, op.idx = eng, fn, dma, self.n
        self.n += 1
        op.signal = False
        op.semval = None
        if dma:
            op.semkey = semkey if semkey is not None else "d_" + (writes[0].name if writes else reads[0].name)
        else:
            op.semkey = eng
        deps = {}
        for b in reads:
            d = b.last_w
            if d is not None:
                deps[d.idx] = (d, True)
        for b in writes:
            d = b.last_w
            if d is not None and d.idx not in deps:
                deps[d.idx] = (d, False)
            for r in b.readers:
                if r.idx not in deps:
                    deps[r.idx] = (r, False)
        keep = []
        for d, raw in deps.values():
            if d.dma or dma or d.eng != eng:
                keep.append(d)
            elif eng == "pe":
                continue
            elif raw:
                keep.append(d)
        for d in keep:
            d.signal = True
        op.deps = keep
        for b in reads:
            if not dma:
                b.readers = [r for r in b.readers if r.dma or r.eng != eng]
            b.readers.append(op)
        for b in writes:
            b.last_w = op
            b.readers = []
        self.ops[eng].append(op)
        return op

    def ins(self, eng, meth, reads, writes, *args, dma=False, semkey=None, **kwargs):
        def fn(e, meth=meth, args=args, kwargs=kwargs):
            return getattr(e, meth)(*args, **kwargs)
        return self.add(eng, fn, reads=reads, writes=writes, dma=dma, semkey=semkey)

    def assign(self):
        counters = {}
        allops = sorted((op for e in self.ENGS for op in self.ops[e]), key=lambda o: o.idx)
        for op in allops:
            if op.signal:
                inc = 16 if op.dma else 1
                counters[op.semkey] = counters.get(op.semkey, 0) + inc
                op.semval = counters[op.semkey]
        return counters

    def emit_engine(self, eng_name, eng, semh):
        known = {}
        for op in self.ops[eng_name]:
            need = {}
            for d in op.deps:
                if d.semval > need.get(d.semkey, 0):
                    need[d.semkey] = d.semval
            for k, v in need.items():
                if known.get(k, 0) < v:
                    eng.wait_ge(semh[k], v)
                    known[k] = v
            ins = op.fn(eng) if op.fn is not None else None
            if op.signal:
                if ins is None:
                    ins = eng.nop()
                ins.then_inc(semh[op.semkey], 16 if op.dma else 1)


def build_program():
    nc = bass.Bass("TRN2", target_bir_lowering=False)
    dt_in = lambda n, shp: nc.dram_tensor(n, shp, F32, kind="ExternalInput").ap()
    x_d = dt_in("x", [NSEQ * S, DM])
    p_d = dt_in("p", [NSEQ * S, 256])
    w_in = dt_in("w_in", [DM, 5128])
    b_forget = dt_in("b_forget", [1, 8])
    b_gate = dt_in("b_gate", [2, DM])
    w_bf = dt_in("w_bf", [512, DM])
    w_bs = dt_in("w_bs", [512, DM])
    w_out = dt_in("w_out", [DM, DM])
    g_mix = dt_in("g_mix", [1, DM])
    g_mlp = dt_in("g_mlp", [1, DM])
    g_ple = dt_in("g_ple", [1, DM])
    g_final = dt_in("g_final", [1, DM])
    w_up = dt_in("w_up", [DM, 4096])
    w_down = dt_in("w_down", [4096, DM])
    w_pg = dt_in("w_pg", [DM, DM])
    w_ple = dt_in("w_ple", [256, DM])
    out_d = nc.dram_tensor("out", [NSEQ * S, DM], F32, kind="ExternalOutput").ap()

    P = Prog()
    es = ExitStack()
    with es:
        ARENA = 205 * 1024
        arena = es.enter_context(nc.sbuf_tensor("arena", [128, ARENA], U8))
        cur = [0]

        def carve(nbytes):
            off = cur[0]
            cur[0] += (nbytes + 63) // 64 * 64
            assert cur[0] <= ARENA, cur[0]
            return off

        def view(off, nbytes, dt, pat=None, **kw):
            v = arena[:, off:off + nbytes].bitcast(dt)
            if pat is not None:
                v = v.rearrange(pat, **kw)
            return v

        def tile(nbytes, dt, pat=None, **kw):
            return view(carve(nbytes), nbytes, dt, pat, **kw)

        ident = tile(256, BF16)
        identf = tile(512, F32)
        ntri = tile(256, BF16)
        tmo = tile(256, BF16)
        nmF = tile(256, BF16)
        nmS = tile(256, BF16)
        zl = tile(256, BF16)
        zr = tile(1024, BF16)
        onesb = tile(256, BF16)
        negb = tile(256, BF16)
        ntriincf = tile(512, F32)
        negonesf = tile(512, F32)
        negf = tile(512, F32)
        bfb = tile(512, F32)
        bgT = tile(64, F32, "p (j c) -> p j c", j=2)
        gt = tile(4096, F32)
        wf = tile(128, BF16, "p (k c) -> p k c", k=8)
        ss = tile(64, F32)
        lnv = tile(64, F32)
        rstd = tile(64, F32)
        cfull = tile(512, F32, "p (t h) -> p t h", t=16)
        cpre = tile(512, F32, "p (t h) -> p t h", t=16)
        totsb = tile(512, F32, "p (t h) -> p t h", t=16)
        tF = tile(512, F32)
        eF = tile(512, F32)
        spF = tile(512, F32)
        B = {}

        def tok(n):
            if n not in B:
                B[n] = Buf(n)
            return B[n]

        NSLOT = 5
        slots = [tile(8192, BF16) for _ in range(NSLOT)]
        hT = tile(32768, BF16, "p (c t) -> p c t", c=8)
        Aoff = carve(32768)
        A = view(Aoff, 32768, BF16, "p (c t) -> p c t", c=8)
        pT = view(Aoff, 8192, BF16, "p (c t) -> p c t", c=2)
        hid = [view(Aoff + 8192 + i * 4096, 4096, BF16, "p (c t) -> p c t", c=4) for i in range(2)]
        Boff = carve(65536)
        V = view(Boff, 24576, BF16, "p (t m c) -> p t m c", t=16, m=4)
        oT = [view(Boff + 24576 + i * 16384, 16384, BF16, "p (c t) -> p c t", c=4) for i in range(2)]
        x1 = view(Boff, 65536, F32, "p (t c) -> p t c", t=16)
        ND = 10
        Doff = carve(ND * 2048)
        Dt = [tok("D%d" % i) for i in range(ND)]

        def dv(chunk, nbytes, dt, pat=None, **kw):
            nch = (nbytes + 2047) // 2048
            assert chunk + nch <= ND
            return view(Doff + chunk * 2048, nbytes, dt, pat, **kw), Dt[chunk:chunk + nch]

        banks = [es.enter_context(nc.psum_tensor("bank%d" % i, [128, 512], F32)) for i in range(8)]
        bkt = [tok("bank%d" % i) for i in range(8)]
        tpb = banks[7][:, :].bitcast(BF16).rearrange("p (c t) -> p c t", c=8)
        rot = {"mm": 0}

        hT_t = [tok("hT%d" % i) for i in range(NB)]
        A_t = [[tok("A%d_%d" % (c, n)) for n in range(4)] for c in range(8)]
        V_t = [tok("V%d" % i) for i in range(NB)]
        oT_t = [[[tok("oT%d_%d_%d" % (b, m, n)) for n in range(4)] for m in range(4)] for b in range(2)]
        x1_t = [tok("x1_%d" % i) for i in range(NB)]
        slot_t = [tok("slot%d" % i) for i in range(NSLOT)]
        allB_old = V_t + [t for b in range(2) for m in range(4) for t in oT_t[b][m]]
        slot_i = [0]

        I = P.ins

        def wload(src2d, kc, cols, into=None, kofs=0):
            if into is None:
                si = slot_i[0] % NSLOT
                slot_i[0] += 1
            else:
                si = into
            kct = 4096 // cols
            vw = slots[si].rearrange("p (k c) -> p k c", k=kct)
            I("pool", "dma_start", [], [slot_t[si]], out=vw[:, kofs:kofs + kc, :],
              in_=src2d.rearrange("(k p) c -> p k c", p=128), dma=True)
            return vw, slot_t[si], si

        def mm_group(bank_i, out_ap, pairs, reads):
            n = len(pairs)
            for i, (l, r) in enumerate(pairs):
                I("pe", "matmul", reads, [bkt[bank_i]], out_ap, lhsT=l, rhs=r, start=(i == 0), stop=(i == n - 1))

        def nextbank(nrot=3):
            b = rot["mm"] % nrot
            rot["mm"] += 1
            return b

        def nextbank2():
            b = 3 + rot["mm2"] % 4
            rot["mm2"] += 1
            return b
        rot["mm2"] = 0

        I("dve", "memset", [], [tok("onesb")], onesb, 1.0)
        I("dve", "memset", [], [tok("negb")], negb, -1.0)
        I("dve", "memset", [], [tok("zl")], zl, 0.0)
        I("dve", "memset", [], [tok("zr")], zr, 0.0)
        I("dve", "memset", [], [tok("negf")], negf, -1.0)
        I("dve", "memset", [], [tok("negonesf")], negonesf, -1.0)

        def asel(out, in_, pat, cm, op, fill, rd, wr):
            I("pool", "affine_select", [tok(rd)], [tok(wr)], out=out, in_=in_, pattern=pat, compare_op=op, fill=fill, base=0,
              channel_multiplier=cm)
        asel(ident, onesb, [[-1, 128]], 1, ALU.is_equal, 0.0, "onesb", "ident")
        asel(ntri, negb, [[-1, 128]], 1, ALU.is_ge, 0.0, "negb", "ntri")
        asel(tmo, negb, [[1, 128]], -1, ALU.is_gt, 0.0, "negb", "tmo")
        asel(nmF, zl, [[1, 128]], -1, ALU.is_ge, NEG, "zl", "nmF")
        asel(nmS, zl, [[1, 128]], -1, ALU.is_gt, NEG, "zl", "nmS")
        asel(ntriincf, negf, [[1, 128]], -1, ALU.is_ge, 0.0, "negf", "ntriincf")
        I("dve", "tensor_copy", [tok("ident")], [tok("identf")], out=identf, in_=ident)
        bf8 = tile(64, F32)
        I("sp", "dma_start", [], [tok("bf8")], out=bf8[:, 0:8], in_=AP(b_forget.tensor, 0, [[0, 128], [1, 8]]), dma=True)
        bf8b = AP(bf8.tensor, bf8.offset, [list(bf8.ap[0]), [0, 16], [1, 8]])
        I("dve", "tensor_copy", [tok("bf8")], [tok("bfb")], out=bfb.rearrange("p (t h) -> p t h", t=16), in_=bf8b)
        I("sp", "dma_start", [], [tok("bgT")], out=bgT, in_=b_gate.rearrange("j (c p) -> p j c", p=128),
          allow_slow_non_contiguous=True, dma=True)

        def load_g(g_ap):
            I("sp", "dma_start", [], [tok("gt")], out=gt, in_=AP(g_ap.tensor, 0, [[0, 128], [1, DM]]), dma=True)

        def rms_stats(xin, xr, tb):
            junk, jt = dv(8, 2048, BF16)
            sst = tok("ss%d" % tb)
            I("act", "activation", xr, jt + [sst], out=junk, in_=xin, func=AF.Square, accum_out=ss[:, tb:tb + 1])
            I("act", "activation", [sst], [tok("lnv%d" % tb)], out=lnv[:, tb:tb + 1], in_=ss[:, tb:tb + 1], func=AF.Ln,
              scale=1.0 / DM, bias=EPS)
            I("act", "activation", [tok("lnv%d" % tb)], [tok("rstd%d" % tb)], out=rstd[:, tb:tb + 1], in_=lnv[:, tb:tb + 1],
              func=AF.Exp, scale=-0.5)

        def norm_T(seq, from_dram, g_ap):
            load_g(g_ap)
            for tb in range(NB):
                if from_dram:
                    xin, xr = dv(4 + 2 * (tb % 2), 4096, F32)
                    r0 = seq * S + tb * 128
                    I("sp", "dma_start", [], xr, out=xin, in_=x_d[r0:r0 + 128, :], dma=True, semkey="d_xin%d" % (tb % 2))
                else:
                    xin, xr = x1[:, tb, :], [x1_t[tb]]
                rms_stats(xin, xr, tb)
                hbf, ht = dv(tb % 2, 2048, BF16)
                I("dve", "scalar_tensor_tensor", xr + [tok("rstd%d" % tb), tok("gt")], ht, out=hbf, in0=xin, scalar=rstd[:, tb:tb + 1],
                  in1=gt, op0=ALU.mult, op1=ALU.mult)
                for c in range(8):
                    I("pe", "transpose", ht + [tok("ident")], [bkt[7]], out=tpb[:, c, :], in_=hbf[:, c * 128:(c + 1) * 128], identity=ident)
                if tb % 2 == 0:
                    I("act", "activation", [bkt[7]], [hT_t[tb]], out=hT[:, :, tb * 128:(tb + 1) * 128], in_=tpb, func=AF.Copy)
                else:
                    I("dve", "tensor_copy", [bkt[7]], [hT_t[tb]], out=hT[:, :, tb * 128:(tb + 1) * 128], in_=tpb)

        def in_proj(br):
            qo, ko, vo = (QF, KF, VF) if br == 0 else (QS, KS, VS)
            Wq, wqt, _ = wload(w_in[:, qo:qo + 512], 8, 512)
            Wk, wkt, _ = wload(w_in[:, ko:ko + 512], 8, 512)
            Wv, wvt, _ = wload(w_in[:, vo:vo + 512], 8, 512)
            if br == 0:
                I("pool", "dma_start", [], [tok("wf")], out=wf, in_=w_in[:, FF:FF + 8].rearrange("(k p) c -> p k c", p=128), dma=True)
            I("dve", "memset", [], V_t + (x1_t if br == 0 else []), V[:, :, :, 64:128], 1.0 if br == 0 else 0.0)
            for which, W, wt in ((0, Wq, wqt), (1, Wk, wkt)):
                for m in range(4):
                    for n in range(4):
                        b = nextbank()
                        mm_group(b, banks[b][:, :], [(W[:, kc, m * 128:(m + 1) * 128], hT[:, kc, n * 512:(n + 1) * 512]) for kc in range(8)],
                                 [wt] + hT_t[4 * n:4 * n + 4])
                        dst = A[:, which * 4 + m, n * 512:(n + 1) * 512]
                        if which == 0:
                            I("act", "activation", [bkt[b]], [A_t[m][n]], out=dst, in_=banks[b][:, :], func=AF.Copy, scale=0.125)
                        else:
                            I("dve", "tensor_copy", [bkt[b]], [A_t[4 + m][n]], out=dst, in_=banks[b][:, :])
            for tb in range(NB):
                b = nextbank()
                mm_group(b, banks[b][:, :], [(hT[:, kc, tb * 128:(tb + 1) * 128], Wv[:, kc, :]) for kc in range(8)], [wvt, hT_t[tb]])
                dst = V[:, tb, :, :].rearrange("p m (a c) -> p m a c", a=3)[:, :, 0:3:2, :]
                src = banks[b][:, :].rearrange("p (m a c) -> p m a c", m=4, a=2)
                if tb % 2 == 0:
                    I("act", "activation", [bkt[b]], [V_t[tb]], out=dst, in_=src, func=AF.Copy)
                else:
                    I("dve", "tensor_copy", [bkt[b]], [V_t[tb]], out=dst, in_=src)

        def forget_tables():
            fb = 7
            for tb in range(NB):
                mm_group(fb, banks[fb][:, tb * 8:(tb + 1) * 8], [(hT[:, kc, tb * 128:(tb + 1) * 128], wf[:, kc, :]) for kc in range(8)],
                         [tok("wf"), hT_t[tb]])
            I("dve", "tensor_tensor", [bkt[fb], tok("bfb")], [tok("tF")], out=tF, in0=banks[fb][:, 0:128], in1=bfb, op=ALU.add)
            I("act", "activation", [tok("tF")], [tok("eF")], out=eF, in_=tF, func=AF.Exp, scale=-1.0)
            I("act", "activation", [tok("eF")], [tok("spF")], out=spF, in_=eF, func=AF.Ln, bias=1.0)
            I("pe", "matmul", [tok("spF"), tok("ntriincf")], [bkt[fb]], banks[fb][:, 128:256], lhsT=ntriincf, rhs=spF, start=True, stop=True)
            I("pe", "matmul", [tok("spF"), tok("negonesf")], [bkt[fb]], banks[fb][:, 256:384], lhsT=negonesf, rhs=spF, start=True, stop=True)
            I("dve", "tensor_copy", [bkt[fb]], [tok("totsb")], out=totsb, in_=banks[fb][:, 256:384].rearrange("p (t h) -> p t h", t=16))
            I("dve", "memset", [], [tok("cpre")], cpre[:, 0, :], 0.0)
            for tb in range(1, NB):
                I("dve", "tensor_tensor", [tok("cpre"), tok("totsb")], [tok("cpre")], out=cpre[:, tb, :], in0=cpre[:, tb - 1, :],
                  in1=totsb[:, tb - 1, :], op=ALU.add)
            I("dve", "tensor_tensor", [bkt[fb], tok("cpre")], [tok("cfull")], out=cfull,
              in0=banks[fb][:, 128:256].rearrange("p (t h) -> p t h", t=16), in1=cpre, op=ALU.add)

        def run_pipe(tiles, stages):
            maxd = max(d for d, _ in stages)
            for t in range(len(tiles) + maxd):
                for d, fn in stages:
                    j = t - d
                    if 0 <= j < len(tiles):
                        fn(tiles[j])

        def fox_block(m, n, par):
            XY = (3 + 2 * par, 4 + 2 * par)
            cts = []
            for hl in range(2):
                h = 2 * m + hl
                ct, ctt = dv(hl + 2 * par, 2048, F32)
                for j in range(4):
                    qb = 4 * n + j
                    I("pe", "matmul", [tok("cfull"), tok("identf")], [bkt[7]], banks[7][:, j * 128:(j + 1) * 128],
                      lhsT=cfull[:, qb, h:h + 1].to_broadcast([128, 128]), rhs=identf, start=True, stop=True)
                I("act", "activation", [bkt[7]], ctt, out=ct, in_=banks[7][:, :], func=AF.Copy)
                cts.append((ct, ctt))
            nkb = 4 * (n + 1)
            tiles = []
            for kb in range(nkb):
                for hl in range(2):
                    i = kb - 4 * n
                    tiles.append(dict(kb=kb, hl=hl, i=i, c0=max(i, 0) * 128, idx=len(tiles)))

            def st_z(T):
                b = nextbank()
                T["zb"] = b
                hl, kb, c0, i = T["hl"], T["kb"], T["c0"], T["i"]
                pl = slice(hl * 64, hl * 64 + 64)
                I("pe", "matmul", [A_t[4 + m][kb // 4], A_t[m][n]], [bkt[b]], banks[b][:, c0:512],
                  lhsT=A[pl, 4 + m, kb * 128:(kb + 1) * 128], rhs=A[pl, m, n * 512 + c0:(n + 1) * 512], start=True, stop=(i < 0))
                if i >= 0:
                    I("pe", "matmul", [tok("ident"), tok("nmF")], [bkt[b]], banks[b][:, c0:c0 + 128], lhsT=ident, rhs=nmF,
                      start=False, stop=True)

            def st_lg(T):
                b, hl, kb, c0 = T["zb"], T["hl"], T["kb"], T["c0"]
                h = 2 * m + hl
                lg, lgt = dv(4 + (T["idx"] % 3), 2048, F32)
                T["lg"], T["lgt"] = lg, lgt
                ct, ctt = cts[hl]
                I("dve", "scalar_tensor_tensor", [bkt[b], tok("cfull")] + ctt, lgt, out=lg[:, c0:512], in0=banks[b][:, c0:512],
                  scalar=cfull[:, kb, h:h + 1], in1=ct[:, c0:512], op0=ALU.subtract, op1=ALU.add)

            def st_exp(T):
                c0 = T["c0"]
                pt, ptt = dv(7 + (T["idx"] % 3), 1024, BF16)
                T["pt"], T["ptt"] = pt, ptt
                I("act", "activation", T["lgt"], ptt, out=pt[:, c0:512], in_=T["lg"][:, c0:512], func=AF.Exp)

            def st_pv(T):
                hl, kb, c0 = T["hl"], T["kb"], T["c0"]
                b = XY[hl]
                I("pe", "matmul", [V_t[kb]] + T["ptt"], [bkt[b]], banks[b][:, c0:512], lhsT=V[:, kb, m, hl * 64:hl * 64 + 128],
                  rhs=T["pt"][:, c0:512], start=(kb == 0), stop=(kb == nkb - 1))

            run_pipe(tiles, [(0, st_z), (0, st_lg), (0, st_exp), (2, st_pv)])
            X, Y = banks[XY[0]], banks[XY[1]]
            den, dent = dv(4, 2048, F32)
            rec, rect = dv(5, 2048, F32)
            I("dve", "tensor_copy", [bkt[XY[0]]], dent, out=den[0:64, :], in_=X[64:128, :])
            I("dve", "tensor_copy", [bkt[XY[1]]], dent, out=den[64:128, :], in_=Y[0:64, :])
            I("act", "activation", dent, rect, out=rec, in_=den, func=AF.Ln)
            I("act", "activation", rect, dent, out=den, in_=rec, func=AF.Exp, scale=-1.0)
            dst = oT[0][:, m, n * 512:(n + 1) * 512]
            I("dve", "tensor_tensor", [bkt[XY[0]]] + dent, [oT_t[0][m][n]], out=dst[0:64, :], in0=X[0:64, :], in1=den[0:64, :], op=ALU.mult)
            I("dve", "tensor_tensor", [bkt[XY[1]]] + dent, [oT_t[0][m][n]], out=dst[64:128, :], in0=Y[64:128, :], in1=den[64:128, :],
              op=ALU.mult)

        def fox_attention():
            k = 0
            for m in range(4):
                for n in range(4):
                    fox_block(m, n, k % 2)
                    k += 1

        def sb_block(m, n, par):
            Bk = (3, 4)
            Ob = 5 + par
            for b in (Bk[0], Bk[1], Ob):
                I("pe", "matmul", [tok("zl"), tok("zr")], [bkt[b]], banks[b][:, :], lhsT=zl, rhs=zr, start=True, stop=False)
            nkb = 4 * (n + 1)
            tiles = []
            for kb in reversed(range(nkb)):
                for hl in range(2):
                    i = kb - 4 * n
                    tiles.append(dict(kb=kb, hl=hl, i=i, c0=max(i, 0) * 128, idx=len(tiles)))
            ntl = len(tiles)

            def st_z(T):
                b = nextbank()
                T["zb"] = b
                hl, kb, c0, i = T["hl"], T["kb"], T["c0"], T["i"]
                pl = slice(hl * 64, hl * 64 + 64)
                I("pe", "matmul", [A_t[4 + m][kb // 4], A_t[m][n]], [bkt[b]], banks[b][:, c0:512],
                  lhsT=A[pl, 4 + m, kb * 128:(kb + 1) * 128], rhs=A[pl, m, n * 512 + c0:(n + 1) * 512], start=True, stop=(i < 0))
                if i >= 0:
                    I("pe", "matmul", [tok("ident"), tok("nmS")], [bkt[b]], banks[b][:, c0:c0 + 128], lhsT=ident, rhs=nmS,
                      start=False, stop=True)

            def st_e(T):
                b, c0 = T["zb"], T["c0"]
                e_, et = dv(T["idx"] % 2, 1024, BF16)
                L_, Lt = dv((2, 3, 8)[T["idx"] % 3], 1024, BF16)
                T["e"], T["et"], T["L"], T["Lt"] = e_, et, L_, Lt
                I("act", "activation", [bkt[b]], et, out=e_[:, c0:512], in_=banks[b][:, c0:512], func=AF.Exp)
                I("act", "activation", et, Lt, out=L_[:, c0:512], in_=e_[:, c0:512], func=AF.Ln, bias=1.0)

            def st_ntri(T):
                hl, c0 = T["hl"], T["c0"]
                b = Bk[hl]
                I("pe", "matmul", [tok("ntri")] + T["Lt"], [bkt[b]], banks[b][:, c0:512], lhsT=ntri, rhs=T["L"][:, c0:512],
                  start=False, stop=False)

            def st_x(T):
                hl, c0 = T["hl"], T["c0"]
                b = Bk[hl]
                X_, Xt = dv(4 + T["idx"] % 2, 1024, BF16)
                T["X"], T["Xt"] = X_, Xt
                I("act", "activation", [bkt[b]], Xt, out=X_[:, c0:512], in_=banks[b][:, c0:512], func=AF.Exp)

            def st_tmo(T):
                hl, c0 = T["hl"], T["c0"]
                b = Bk[hl]
                I("pe", "matmul", [tok("tmo")] + T["Lt"], [bkt[b]], banks[b][:, c0:512], lhsT=tmo, rhs=T["L"][:, c0:512],
                  start=False, stop=False)

            def st_at(T):
                c0 = T["c0"]
                AT, ATt = dv(6 + T["idx"] % 2, 1024, BF16)
                T["AT"], T["ATt"] = AT, ATt
                I("dve", "tensor_tensor", T["et"] + T["Xt"], ATt, out=AT[:, c0:512], in0=T["e"][:, c0:512], in1=T["X"][:, c0:512],
                  op=ALU.mult)

            def st_pv(T):
                hl, kb, c0 = T["hl"], T["kb"], T["c0"]
                I("pe", "matmul", [V_t[kb]] + T["ATt"], [bkt[Ob]], banks[Ob][:, c0:512], lhsT=V[:, kb, m, hl * 64:hl * 64 + 128],
                  rhs=T["AT"][:, c0:512], start=False, stop=(T["idx"] == ntl - 1))

            run_pipe(tiles, [(0, st_z), (0, st_e), (1, st_ntri), (1, st_x), (2, st_tmo), (1, st_at), (2, st_pv)])
            I("dve", "tensor_copy", [bkt[Ob]], [oT_t[1][m][n]], out=oT[1][:, m, n * 512:(n + 1) * 512], in_=banks[Ob][:, :])

        def sb_attention():
            k = 0
            for m in range(4):
                for n in range(4):
                    sb_block(m, n, k % 2)
                    k += 1

        def gates_merge():
            for cg in range(2):
                Wga, gat, _ = wload(w_in[:, GA + cg * 512:GA + (cg + 1) * 512], 8, 512)
                Wgb, gbt, _ = wload(w_in[:, GB + cg * 512:GB + (cg + 1) * 512], 8, 512)
                Wbr, brt, si = wload(w_bf[:, cg * 512:(cg + 1) * 512], 4, 512)
                wload(w_bs[:, cg * 512:(cg + 1) * 512], 4, 512, into=si, kofs=4)
                for c in range(4):
                    cc = cg * 4 + c
                    cs = slice(c * 128, (c + 1) * 128)
                    for n in range(4):
                        ns = slice(n * 512, (n + 1) * 512)
                        sg = []
                        for j, (W, wt) in enumerate(((Wga, gat), (Wgb, gbt))):
                            b = nextbank()
                            mm_group(b, banks[b][:, :], [(W[:, kc, cs], hT[:, kc, ns]) for kc in range(8)], [wt] + hT_t[4 * n:4 * n + 4])
                            sgv, sgt = dv(4 + j, 2048, F32)
                            I("act", "activation", [bkt[b], tok("bgT")], sgt, out=sgv, in_=banks[b][:, :], func=AF.Sigmoid,
                              bias=bgT[:, j, cc:cc + 1])
                            sg.append((sgv, sgt))
                        tt = []
                        for j in range(2):
                            b = 3 + j
                            mm_group(b, banks[b][:, :], [(Wbr[:, 4 * j + kc, cs], oT[j][:, kc, ns]) for kc in range(4)],
                                     [brt] + [oT_t[j][kc][n] for kc in range(4)])
                            tv, tvt = dv(6 + j, 2048, F32)
                            I("dve", "tensor_tensor", [bkt[b]] + sg[j][1], tvt, out=tv, in0=banks[b][:, :], in1=sg[j][0], op=ALU.mult)
                            tt.append((tv, tvt))
                        I("dve", "tensor_tensor", tt[0][1] + tt[1][1], [A_t[cc][n]], out=A[:, cc, ns], in0=tt[0][0], in1=tt[1][0], op=ALU.add)

        def out_proj(seq):
            Wo = [wload(w_out[:, ch * 512:(ch + 1) * 512], 8, 512) for ch in range(2)]
            for tb in range(NB):
                xs, xst = dv(2 * (tb % 2), 4096, F32)
                r0 = seq * S + tb * 128
                I("sp", "dma_start", [], xst, out=xs, in_=x_d[r0:r0 + 128, :], dma=True, semkey="d_xs%d" % (tb % 2))
                for ch in range(2):
                    b = nextbank()
                    W, wt, _ = Wo[ch]
                    mm_group(b, banks[b][:, :], [(A[:, kc, tb * 128:(tb + 1) * 128], W[:, kc, :]) for kc in range(8)],
                             [wt] + [A_t[kc][tb // 4] for kc in range(8)])
                    wr = [x1_t[tb]] + (allB_old if (tb == 0 and ch == 0) else [])
                    I("dve", "tensor_tensor", [bkt[b]] + xst, wr, out=x1[:, tb, ch * 512:(ch + 1) * 512], in0=banks[b][:, :],
                      in1=xs[:, ch * 512:(ch + 1) * 512], op=ALU.add)

        def mlp():
            allA = [t for c in range(8) for t in A_t[c]]
            for g in range(8):
                Wu, wut, _ = wload(w_up[:, g * 512:(g + 1) * 512], 8, 512)
                Wd, wdt, _ = wload(w_down[g * 512:(g + 1) * 512, :], 4, 1024)
                for n in range(4):
                    hsel = (g * 4 + n) % 2
                    hd = hid[hsel]
                    hdt = [tok("hid%d_%d" % (hsel, c)) for c in range(4)]
                    for c in range(4):
                        b = nextbank()
                        mm_group(b, banks[b][:, :], [(Wu[:, kc, c * 128:(c + 1) * 128], hT[:, kc, n * 512:(n + 1) * 512]) for kc in range(8)],
                                 [wut] + hT_t[4 * n:4 * n + 4])
                        sq, sqt = dv(4 + (c % 2), 1024, BF16)
                        I("act", "activation", [bkt[b]], sqt, out=sq, in_=banks[b][:, :], func=AF.Square)
                        wr = [hdt[c]] + (allA if (g == 0 and n == 0 and c == 0) else [])
                        I("dve", "scalar_tensor_tensor", [bkt[b]] + sqt, wr, out=hd[:, c, :], in0=banks[b][:, :], scalar=0.0, in1=sq,
                          op0=ALU.is_gt, op1=ALU.mult)
                    for tl in range(4):
                        tb = 4 * n + tl
                        for ch in range(2):
                            b = nextbank2()
                            mm_group(b, banks[b][:, :], [(hd[:, kc, tl * 128:(tl + 1) * 128], Wd[:, kc, ch * 512:(ch + 1) * 512]) for kc in range(4)],
                                     [wdt] + hdt)
                            I("dve", "tensor_tensor", [bkt[b], x1_t[tb]], [x1_t[tb]], out=x1[:, tb, ch * 512:(ch + 1) * 512],
                              in0=banks[b][:, :], in1=x1[:, tb, ch * 512:(ch + 1) * 512], op=ALU.add)

        def ple(seq):
            allA = [t for c in range(8) for t in A_t[c]]
            pT_t = [tok("pT%d" % tb) for tb in range(NB)]
            for tb in range(NB):
                pin, pint = dv(4 + (tb % 2), 1024, F32)
                r0 = seq * S + tb * 128
                I("sp", "dma_start", [], pint, out=pin, in_=p_d[r0:r0 + 128, :], dma=True, semkey="d_pin%d" % (tb % 2))
                pbf, pbt = dv(6 + (tb % 2), 512, BF16)
                I("dve", "tensor_copy", pint, pbt, out=pbf, in_=pin)
                for c in range(2):
                    I("pe", "transpose", pbt + [tok("ident")], [bkt[7]], out=tpb[:, c, :], in_=pbf[:, c * 128:(c + 1) * 128], identity=ident)
                I("act", "activation", [bkt[7]], [pT_t[tb]] + (allA if tb == 0 else []), out=pT[:, :, tb * 128:(tb + 1) * 128],
                  in_=tpb[:, 0:2, :], func=AF.Copy)
            Wpg = [wload(w_pg[:, ch * 512:(ch + 1) * 512], 8, 512) for ch in range(2)]
            Wp, wpt, _ = wload(w_ple[:, :], 2, 1024)
            for tb in range(NB):
                for ch in range(2):
                    b = nextbank()
                    W, wt, _ = Wpg[ch]
                    mm_group(b, banks[b][:, :], [(hT[:, kc, tb * 128:(tb + 1) * 128], W[:, kc, :]) for kc in range(8)], [wt, hT_t[tb]])
                    sgv, sgt = dv(8 + (ch % 2), 2048, F32)
                    I("act", "activation", [bkt[b]], sgt, out=sgv, in_=banks[b][:, :], func=AF.Sigmoid)
                    b2 = nextbank2()
                    mm_group(b2, banks[b2][:, :], [(pT[:, kc, tb * 128:(tb + 1) * 128], Wp[:, kc, ch * 512:(ch + 1) * 512]) for kc in range(2)],
                             [wpt, pT_t[tb]])
                    I("dve", "tensor_tensor", [bkt[b2]] + sgt, sgt, out=sgv, in0=banks[b2][:, :], in1=sgv, op=ALU.mult)
                    I("dve", "tensor_tensor", sgt + [x1_t[tb]], [x1_t[tb]], out=x1[:, tb, ch * 512:(ch + 1) * 512], in0=sgv,
                      in1=x1[:, tb, ch * 512:(ch + 1) * 512], op=ALU.add)

        store_t = []

        def final_norm(seq):
            load_g(g_final)
            for tb in range(NB):
                xin = x1[:, tb, :]
                rms_stats(xin, [x1_t[tb]], tb)
                ob, obt = dv(2 * (tb % 2), 4096, F32)
                I("dve", "scalar_tensor_tensor", [x1_t[tb], tok("rstd%d" % tb), tok("gt")], obt, out=ob, in0=xin, scalar=rstd[:, tb:tb + 1],
                  in1=gt, op0=ALU.mult, op1=ALU.mult)
                r0 = seq * S + tb * 128
                st = tok("store%d_%d" % (seq, tb))
                store_t.append(st)
                I("sp", "dma_start", obt, [st], out=out_d[r0:r0 + 128, :], in_=ob, dma=True, semkey="d_store")

        for seq in range(NSEQ):
            norm_T(seq, True, g_mix)
            in_proj(0)
            forget_tables()
            fox_attention()
            in_proj(1)
            sb_attention()
            gates_merge()
            out_proj(seq)
            norm_T(seq, False, g_mlp)
            mlp()
            norm_T(seq, False, g_ple)
            ple(seq)
            final_norm(seq)
        P.add("sp", None, reads=store_t)

        counters = P.assign()
        semh = {k: es.enter_context(nc.semaphore("s_" + k)) for k in counters}
        with nc.Block() as block:
            @block.tensor
            def _(e):
                P.emit_engine("pe", e, semh)

            @block.scalar
            def _(e):
                P.emit_engine("act", e, semh)

            @block.vector
            def _(e):
                P.emit_engine("dve", e, semh)

            @block.gpsimd
            def _(e):
                P.emit_engine("pool", e, semh)

            @block.sync
            def _(e):
                P.emit_engine("sp", e, semh)
    return nc


def kernel(x, p, g_mix, w_in, b_forget, b_gate, w_branch_fox, w_branch_sb, w_out,
           g_mlp, w_up, w_down, g_ple, w_ple_gate, w_ple, g_final):
    f = lambda a: np.ascontiguousarray(np.asarray(a, dtype=np.float32))
    x = f(x)
    p = f(p)
    common = {
        "w_in": f(w_in)[0], "b_forget": f(b_forget).reshape(1, 8), "b_gate": f(b_gate)[0],
        "w_bf": f(w_branch_fox)[0], "w_bs": f(w_branch_sb)[0], "w_out": f(w_out)[0],
        "g_mix": f(g_mix).reshape(1, DM), "g_mlp": f(g_mlp).reshape(1, DM), "g_ple": f(g_ple).reshape(1, DM),
        "g_final": f(g_final).reshape(1, DM), "w_up": f(w_up)[0], "w_down": f(w_down)[0],
        "w_pg": f(w_ple_gate)[0], "w_ple": f(w_ple)[0],
    }
    in_maps = []
    for c in range(NCORES):
        m = dict(common)
        m["x"] = x[c * NSEQ:(c + 1) * NSEQ].reshape(NSEQ * S, DM)
        m["p"] = p[0, c * NSEQ:(c + 1) * NSEQ].reshape(NSEQ * S, 256)
        in_maps.append(m)
    nc = build_program()
    res = run_bass_kernel_spmd(nc, in_maps, core_ids=list(range(NCORES)))
    out = np.stack([np.asarray(r["out"]).reshape(NSEQ, S, DM) for r in res.results], axis=0)
    return out.reshape(NCORES * NSEQ, S, DM).astype(np.float32)
```

```python
from contextlib import ExitStack
import numpy as np
import concourse.bass as bass
import concourse.mybir as mybir
from concourse.bass_utils import run_bass_kernel_spmd
from concourse.ap import AP

F32 = mybir.dt.float32
BF16 = mybir.dt.bfloat16
U8 = mybir.dt.uint8
AF = mybir.ActivationFunctionType
ALU = mybir.AluOpType

NCORES = 8
S = 2048
DM = 1024
NB = 16
NSEQ = 2
EPS = 1e-6
QF, KF, VF, FF, QS, KS, VS, GA, GB = 0, 512, 1024, 1536, 1544, 2056, 2568, 3080, 4104
NEG = -30000.0


class Buf:
    __slots__ = ("name", "last_w", "readers")

    def __init__(self, name):
        self.name = name
        self.last_w = None
        self.readers = []


class Op:
    __slots__ = ("eng", "fn", "deps", "signal", "semkey", "semval", "dma", "idx")


class Prog:
    ENGS = ("pe", "act", "dve", "pool", "sp")

    def __init__(self):
        self.ops = {e: [] for e in self.ENGS}
        self.n = 0

    def add(self, eng, fn, reads=(), writes=(), dma=False, semkey=None):
        op = Op()
        op.eng, op.fn, op.dma, op.idx = eng, fn, dma, self.n
        self.n += 1
        op.signal = False
        op.semval = None
        if dma:
            op.semkey = semkey if semkey is not None else "d_" + (writes[0].name if writes else reads[0].name)
        else:
            op.semkey = eng
        deps = {}
        for b in reads:
            d = b.last_w
            if d is not None:
                deps[d.idx] = (d, True)
        for b in writes:
            d = b.last_w
            if d is not None and d.idx not in deps:
                deps[d.idx] = (d, False)
            for r in b.readers:
                if r.idx not in deps:
                    deps[r.idx] = (r, False)
        keep = []
        for d, raw in deps.values():
            if d.dma or dma or d.eng != eng:
                keep.append(d)
            elif eng == "pe":
                continue
            elif raw:
                keep.append(d)
        for d in keep:
            d.signal = True
        op.deps = keep
        for b in reads:
            if not dma:
                b.readers = [r for r in b.readers if r.dma or r.eng != eng]
            b.readers.append(op)
        for b in writes:
            b.last_w = op
            b.readers = []
        self.ops[eng].append(op)
        return op

    def ins(self, eng, meth, reads, writes, *args, dma=False, semkey=None, **kwargs):
        def fn(e, meth=meth, args=args, kwargs=kwargs):
            return getattr(e, meth)(*args, **kwargs)
        return self.add(eng, fn, reads=reads, writes=writes, dma=dma, semkey=semkey)

    def assign(self):
        counters = {}
        allops = sorted((op for e in self.ENGS for op in self.ops[e]), key=lambda o: o.idx)
        for op in allops:
            if op.signal:
                inc = 16 if op.dma else 1
                counters[op.semkey] = counters.get(op.semkey, 0) + inc
                op.semval = counters[op.semkey]
        return counters

    def emit_engine(self, eng_name, eng, semh):
        known = {}
        for op in self.ops[eng_name]:
            need = {}
            for d in op.deps:
                if d.semval > need.get(d.semkey, 0):
                    need[d.semkey] = d.semval
            todo = []
            for k, v in need.items():
                if known.get(k, 0) < v:
                    todo.append((k, v))
                    known[k] = v
            attach = todo.pop() if (todo and op.fn is not None) else None
            for k, v in todo:
                eng.wait_ge(semh[k], v)
            ins = op.fn(eng) if op.fn is not None else None
            if attach is not None:
                ins._wait_ge(semh[attach[0]], attach[1])
            if op.signal:
                if ins is None:
                    ins = eng.nop()
                ins.then_inc(semh[op.semkey], 16 if op.dma else 1)


def build_program():
    nc = bass.Bass("TRN2", target_bir_lowering=False)
    dt_in = lambda n, shp: nc.dram_tensor(n, shp, F32, kind="ExternalInput").ap()
    x_d = dt_in("x", [NSEQ * S, DM])
    p_d = dt_in("p", [NSEQ * S, 256])
    w_in = dt_in("w_in", [DM, 5128])
    b_forget = dt_in("b_forget", [1, 8])
    b_gate = dt_in("b_gate", [2, DM])
    w_bf = dt_in("w_bf", [512, DM])
    w_bs = dt_in("w_bs", [512, DM])
    w_out = dt_in("w_out", [DM, DM])
    g_mix = dt_in("g_mix", [1, DM])
    g_mlp = dt_in("g_mlp", [1, DM])
    g_ple = dt_in("g_ple", [1, DM])
    g_final = dt_in("g_final", [1, DM])
    w_up = dt_in("w_up", [DM, 4096])
    w_down = dt_in("w_down", [4096, DM])
    w_pg = dt_in("w_pg", [DM, DM])
    w_ple = dt_in("w_ple", [256, DM])
    out_d = nc.dram_tensor("out", [NSEQ * S, DM], F32, kind="ExternalOutput").ap()

    P = Prog()
    es = ExitStack()
    with es:
        ARENA = 205 * 1024
        arena = es.enter_context(nc.sbuf_tensor("arena", [128, ARENA], U8))
        cur = [0]

        def carve(nbytes):
            off = cur[0]
            cur[0] += (nbytes + 63) // 64 * 64
            assert cur[0] <= ARENA, cur[0]
            return off

        def view(off, nbytes, dt, pat=None, **kw):
            v = arena[:, off:off + nbytes].bitcast(dt)
            if pat is not None:
                v = v.rearrange(pat, **kw)
            return v

        def tile(nbytes, dt, pat=None, **kw):
            return view(carve(nbytes), nbytes, dt, pat, **kw)

        ident = tile(256, BF16)
        identf = tile(512, F32)
        ntri = tile(256, BF16)
        tmo = tile(256, BF16)
        nmF = tile(256, BF16)
        nmS = tile(256, BF16)
        zl = tile(256, BF16)
        zr = tile(1024, BF16)
        onesb = tile(256, BF16)
        negb = tile(256, BF16)
        ntriincf = tile(512, F32)
        negonesf = tile(512, F32)
        negf = tile(512, F32)
        bfb = tile(512, F32)
        bgT = tile(64, F32, "p (j c) -> p j c", j=2)
        gt = tile(4096, F32)
        wf = tile(128, BF16, "p (k c) -> p k c", k=8)
        ss = tile(64, F32)
        lnv = tile(64, F32)
        rstd = tile(64, F32)
        cfull = tile(512, F32, "p (t h) -> p t h", t=16)
        cpre = tile(512, F32, "p (t h) -> p t h", t=16)
        totsb = tile(512, F32, "p (t h) -> p t h", t=16)
        tF = tile(512, F32)
        eF = tile(512, F32)
        spF = tile(512, F32)
        B = {}

        def tok(n):
            if n not in B:
                B[n] = Buf(n)
            return B[n]

        NSLOT = 5
        slots = [tile(8192, BF16) for _ in range(NSLOT)]
        hT = tile(32768, BF16, "p (c t) -> p c t", c=8)
        Aoff = carve(32768)
        A = view(Aoff, 32768, BF16, "p (c t) -> p c t", c=8)
        pT = view(Aoff, 8192, BF16, "p (c t) -> p c t", c=2)
        hid = [view(Aoff + 8192 + i * 4096, 4096, BF16, "p (c t) -> p c t", c=4) for i in range(2)]
        Boff = carve(65536)
        V = view(Boff, 24576, BF16, "p (t m c) -> p t m c", t=16, m=4)
        oT = [view(Boff + 24576 + i * 16384, 16384, BF16, "p (c t) -> p c t", c=4) for i in range(2)]
        x1 = view(Boff, 65536, F32, "p (t c) -> p t c", t=16)
        ND = 10
        Doff = carve(ND * 2048)
        Dt = [tok("D%d" % i) for i in range(ND)]

        def dv(chunk, nbytes, dt, pat=None, **kw):
            nch = (nbytes + 2047) // 2048
            assert chunk + nch <= ND
            return view(Doff + chunk * 2048, nbytes, dt, pat, **kw), Dt[chunk:chunk + nch]

        banks = [es.enter_context(nc.psum_tensor("bank%d" % i, [128, 512], F32)) for i in range(8)]
        bkt = [tok("bank%d" % i) for i in range(8)]
        tpb = banks[7][:, :].bitcast(BF16).rearrange("p (c t) -> p c t", c=8)
        rot = {"mm": 0}

        hT_t = [tok("hT%d" % i) for i in range(NB)]
        A_t = [[tok("A%d_%d" % (c, n)) for n in range(4)] for c in range(8)]
        V_t = [tok("V%d" % i) for i in range(NB)]
        oT_t = [[[tok("oT%d_%d_%d" % (b, m, n)) for n in range(4)] for m in range(4)] for b in range(2)]
        x1_t = [tok("x1_%d" % i) for i in range(NB)]
        slot_t = [tok("slot%d" % i) for i in range(NSLOT)]
        allB_old = V_t + [t for b in range(2) for m in range(4) for t in oT_t[b][m]]
        slot_i = [0]

        I = P.ins

        def wload(src2d, kc, cols, into=None, kofs=0):
            if into is None:
                si = slot_i[0] % NSLOT
                slot_i[0] += 1
            else:
                si = into
            kct = 4096 // cols
            vw = slots[si].rearrange("p (k c) -> p k c", k=kct)
            I("pool", "dma_start", [], [slot_t[si]], out=vw[:, kofs:kofs + kc, :],
              in_=src2d.rearrange("(k p) c -> p k c", p=128), dma=True)
            return vw, slot_t[si], si

        def mm_group(bank_i, out_ap, pairs, reads):
            n = len(pairs)
            for i, (l, r) in enumerate(pairs):
                I("pe", "matmul", reads, [bkt[bank_i]], out_ap, lhsT=l, rhs=r, start=(i == 0), stop=(i == n - 1))

        def nextbank(nrot=3):
            b = rot["mm"] % nrot
            rot["mm"] += 1
            return b

        def nextbank2():
            b = 3 + rot["mm2"] % 4
            rot["mm2"] += 1
            return b
        rot["mm2"] = 0

        I("dve", "memset", [], [tok("onesb")], onesb, 1.0)
        I("dve", "memset", [], [tok("negb")], negb, -1.0)
        I("dve", "memset", [], [tok("zl")], zl, 0.0)
        I("dve", "memset", [], [tok("zr")], zr, 0.0)
        I("dve", "memset", [], [tok("negf")], negf, -1.0)
        I("dve", "memset", [], [tok("negonesf")], negonesf, -1.0)

        def asel(out, in_, pat, cm, op, fill, rd, wr):
            I("pool", "affine_select", [tok(rd)], [tok(wr)], out=out, in_=in_, pattern=pat, compare_op=op, fill=fill, base=0,
              channel_multiplier=cm)
        asel(ident, onesb, [[-1, 128]], 1, ALU.is_equal, 0.0, "onesb", "ident")
        asel(ntri, negb, [[-1, 128]], 1, ALU.is_ge, 0.0, "negb", "ntri")
        asel(tmo, negb, [[1, 128]], -1, ALU.is_gt, 0.0, "negb", "tmo")
        asel(nmF, zl, [[1, 128]], -1, ALU.is_ge, NEG, "zl", "nmF")
        asel(nmS, zl, [[1, 128]], -1, ALU.is_gt, NEG, "zl", "nmS")
        asel(ntriincf, negf, [[1, 128]], -1, ALU.is_ge, 0.0, "negf", "ntriincf")
        I("dve", "tensor_copy", [tok("ident")], [tok("identf")], out=identf, in_=ident)
        bf8 = tile(64, F32)
        I("sp", "dma_start", [], [tok("bf8")], out=bf8[:, 0:8], in_=AP(b_forget.tensor, 0, [[0, 128], [1, 8]]), dma=True)
        bf8b = AP(bf8.tensor, bf8.offset, [list(bf8.ap[0]), [0, 16], [1, 8]])
        I("dve", "tensor_copy", [tok("bf8")], [tok("bfb")], out=bfb.rearrange("p (t h) -> p t h", t=16), in_=bf8b)
        I("sp", "dma_start", [], [tok("bgT")], out=bgT, in_=b_gate.rearrange("j (c p) -> p j c", p=128),
          allow_slow_non_contiguous=True, dma=True)

        def load_g(g_ap):
            I("sp", "dma_start", [], [tok("gt")], out=gt, in_=AP(g_ap.tensor, 0, [[0, 128], [1, DM]]), dma=True)

        def rms_stats(xin, xr, tb):
            junk, jt = dv(8, 2048, BF16)
            sst = tok("ss%d" % tb)
            I("act", "activation", xr, jt + [sst], out=junk, in_=xin, func=AF.Square, accum_out=ss[:, tb:tb + 1])
            I("act", "activation", [sst], [tok("lnv%d" % tb)], out=lnv[:, tb:tb + 1], in_=ss[:, tb:tb + 1], func=AF.Ln,
              scale=1.0 / DM, bias=EPS)
            I("act", "activation", [tok("lnv%d" % tb)], [tok("rstd%d" % tb)], out=rstd[:, tb:tb + 1], in_=lnv[:, tb:tb + 1],
              func=AF.Exp, scale=-0.5)

        def norm_T(seq, from_dram, g_ap):
            load_g(g_ap)
            for tb in range(NB):
                if from_dram:
                    xin, xr = dv(4 + 2 * (tb % 2), 4096, F32)
                    r0 = seq * S + tb * 128
                    I("sp", "dma_start", [], xr, out=xin, in_=x_d[r0:r0 + 128, :], dma=True, semkey="d_xin%d" % (tb % 2))
                else:
                    xin, xr = x1[:, tb, :], [x1_t[tb]]
                rms_stats(xin, xr, tb)
                hbf, ht = dv(tb % 2, 2048, BF16)
                I("dve", "scalar_tensor_tensor", xr + [tok("rstd%d" % tb), tok("gt")], ht, out=hbf, in0=xin, scalar=rstd[:, tb:tb + 1],
                  in1=gt, op0=ALU.mult, op1=ALU.mult)
                for c in range(8):
                    I("pe", "transpose", ht + [tok("ident")], [bkt[7]], out=tpb[:, c, :], in_=hbf[:, c * 128:(c + 1) * 128], identity=ident)
                if tb % 2 == 0:
                    I("act", "activation", [bkt[7]], [hT_t[tb]], out=hT[:, :, tb * 128:(tb + 1) * 128], in_=tpb, func=AF.Copy)
                else:
                    I("dve", "tensor_copy", [bkt[7]], [hT_t[tb]], out=hT[:, :, tb * 128:(tb + 1) * 128], in_=tpb)

        def in_proj(br):
            qo, ko, vo = (QF, KF, VF) if br == 0 else (QS, KS, VS)
            Wq, wqt, _ = wload(w_in[:, qo:qo + 512], 8, 512)
            Wk, wkt, _ = wload(w_in[:, ko:ko + 512], 8, 512)
            Wv, wvt, _ = wload(w_in[:, vo:vo + 512], 8, 512)
            if br == 0:
                I("pool", "dma_start", [], [tok("wf")], out=wf, in_=w_in[:, FF:FF + 8].rearrange("(k p) c -> p k c", p=128), dma=True)
            I("dve", "memset", [], V_t + (x1_t if br == 0 else []), V[:, :, :, 64:128], 1.0 if br == 0 else 0.0)
            for which, W, wt in ((0, Wq, wqt), (1, Wk, wkt)):
                for m in range(4):
                    for n in range(4):
                        b = nextbank()
                        mm_group(b, banks[b][:, :], [(W[:, kc, m * 128:(m + 1) * 128], hT[:, kc, n * 512:(n + 1) * 512]) for kc in range(8)],
                                 [wt] + hT_t[4 * n:4 * n + 4])
                        dst = A[:, which * 4 + m, n * 512:(n + 1) * 512]
                        if which == 0:
                            I("act", "activation", [bkt[b]], [A_t[m][n]], out=dst, in_=banks[b][:, :], func=AF.Copy, scale=0.125)
                        else:
                            I("dve", "tensor_copy", [bkt[b]], [A_t[4 + m][n]], out=dst, in_=banks[b][:, :])
            for tb in range(NB):
                b = nextbank()
                mm_group(b, banks[b][:, :], [(hT[:, kc, tb * 128:(tb + 1) * 128], Wv[:, kc, :]) for kc in range(8)], [wvt, hT_t[tb]])
                dst = V[:, tb, :, :].rearrange("p m (a c) -> p m a c", a=3)[:, :, 0:3:2, :]
                src = banks[b][:, :].rearrange("p (m a c) -> p m a c", m=4, a=2)
                if tb % 2 == 0:
                    I("act", "activation", [bkt[b]], [V_t[tb]], out=dst, in_=src, func=AF.Copy)
                else:
                    I("dve", "tensor_copy", [bkt[b]], [V_t[tb]], out=dst, in_=src)

        def forget_tables():
            fb = 7
            for tb in range(NB):
                mm_group(fb, banks[fb][:, tb * 8:(tb + 1) * 8], [(hT[:, kc, tb * 128:(tb + 1) * 128], wf[:, kc, :]) for kc in range(8)],
                         [tok("wf"), hT_t[tb]])
            I("dve", "tensor_tensor", [bkt[fb], tok("bfb")], [tok("tF")], out=tF, in0=banks[fb][:, 0:128], in1=bfb, op=ALU.add)
            I("act", "activation", [tok("tF")], [tok("eF")], out=eF, in_=tF, func=AF.Exp, scale=-1.0)
            I("act", "activation", [tok("eF")], [tok("spF")], out=spF, in_=eF, func=AF.Ln, bias=1.0)
            I("pe", "matmul", [tok("spF"), tok("ntriincf")], [bkt[fb]], banks[fb][:, 128:256], lhsT=ntriincf, rhs=spF, start=True, stop=True)
            I("pe", "matmul", [tok("spF"), tok("negonesf")], [bkt[fb]], banks[fb][:, 256:384], lhsT=negonesf, rhs=spF, start=True, stop=True)
            I("dve", "tensor_copy", [bkt[fb]], [tok("totsb")], out=totsb, in_=banks[fb][:, 256:384].rearrange("p (t h) -> p t h", t=16))
            I("dve", "memset", [], [tok("cpre")], cpre[:, 0, :], 0.0)
            for tb in range(1, NB):
                I("dve", "tensor_tensor", [tok("cpre"), tok("totsb")], [tok("cpre")], out=cpre[:, tb, :], in0=cpre[:, tb - 1, :],
                  in1=totsb[:, tb - 1, :], op=ALU.add)
            I("dve", "tensor_tensor", [bkt[fb], tok("cpre")], [tok("cfull")], out=cfull,
              in0=banks[fb][:, 128:256].rearrange("p (t h) -> p t h", t=16), in1=cpre, op=ALU.add)

        def run_pipe(tiles, stages):
            maxd = max(d for d, _ in stages)
            for t in range(len(tiles) + maxd):
                for d, fn in stages:
                    j = t - d
                    if 0 <= j < len(tiles):
                        fn(tiles[j])

        def fox_block(m, n, par):
            XY = (3 + 2 * par, 4 + 2 * par)
            cts = []
            for hl in range(2):
                h = 2 * m + hl
                ct, ctt = dv(hl + 2 * par, 2048, F32)
                for j in range(4):
                    qb = 4 * n + j
                    I("pe", "matmul", [tok("cfull"), tok("identf")], [bkt[7]], banks[7][:, j * 128:(j + 1) * 128],
                      lhsT=cfull[:, qb, h:h + 1].to_broadcast([128, 128]), rhs=identf, start=True, stop=True)
                I("act", "activation", [bkt[7]], ctt, out=ct, in_=banks[7][:, :], func=AF.Copy)
                cts.append((ct, ctt))
            nkb = 4 * (n + 1)
            tiles = []
            for kb in range(nkb):
                for hl in range(2):
                    i = kb - 4 * n
                    tiles.append(dict(kb=kb, hl=hl, i=i, c0=max(i, 0) * 128, idx=len(tiles)))

            def st_z(T):
                b = nextbank()
                T["zb"] = b
                hl, kb, c0, i = T["hl"], T["kb"], T["c0"], T["i"]
                pl = slice(hl * 64, hl * 64 + 64)
                I("pe", "matmul", [A_t[4 + m][kb // 4], A_t[m][n]], [bkt[b]], banks[b][:, c0:512],
                  lhsT=A[pl, 4 + m, kb * 128:(kb + 1) * 128], rhs=A[pl, m, n * 512 + c0:(n + 1) * 512], start=True, stop=(i < 0))
                if i >= 0:
                    I("pe", "matmul", [tok("ident"), tok("nmF")], [bkt[b]], banks[b][:, c0:c0 + 128], lhsT=ident, rhs=nmF,
                      start=False, stop=True)

            def st_lg(T):
                b, hl, kb, c0 = T["zb"], T["hl"], T["kb"], T["c0"]
                h = 2 * m + hl
                lg, lgt = dv(4 + (T["idx"] % 3), 2048, F32)
                T["lg"], T["lgt"] = lg, lgt
                ct, ctt = cts[hl]
                I("dve", "scalar_tensor_tensor", [bkt[b], tok("cfull")] + ctt, lgt, out=lg[:, c0:512], in0=banks[b][:, c0:512],
                  scalar=cfull[:, kb, h:h + 1], in1=ct[:, c0:512], op0=ALU.subtract, op1=ALU.add)

            def st_exp(T):
                c0 = T["c0"]
                pt, ptt = dv(7 + (T["idx"] % 3), 1024, BF16)
                T["pt"], T["ptt"] = pt, ptt
                I("act", "activation", T["lgt"], ptt, out=pt[:, c0:512], in_=T["lg"][:, c0:512], func=AF.Exp)

            def st_pv(T):
                hl, kb, c0 = T["hl"], T["kb"], T["c0"]
                b = XY[hl]
                I("pe", "matmul", [V_t[kb]] + T["ptt"], [bkt[b]], banks[b][:, c0:512], lhsT=V[:, kb, m, hl * 64:hl * 64 + 128],
                  rhs=T["pt"][:, c0:512], start=(kb == 0), stop=(kb == nkb - 1))

            run_pipe(tiles, [(0, st_z), (0, st_lg), (0, st_exp), (2, st_pv)])
            X, Y = banks[XY[0]], banks[XY[1]]
            den, dent = dv(4, 2048, F32)
            rec, rect = dv(5, 2048, F32)
            I("dve", "tensor_copy", [bkt[XY[0]]], dent, out=den[0:64, :], in_=X[64:128, :])
            I("dve", "tensor_copy", [bkt[XY[1]]], dent, out=den[64:128, :], in_=Y[0:64, :])
            I("act", "activation", dent, rect, out=rec, in_=den, func=AF.Ln)
            I("act", "activation", rect, dent, out=den, in_=rec, func=AF.Exp, scale=-1.0)
            dst = oT[0][:, m, n * 512:(n + 1) * 512]
            I("dve", "tensor_tensor", [bkt[XY[0]]] + dent, [oT_t[0][m][n]], out=dst[0:64, :], in0=X[0:64, :], in1=den[0:64, :], op=ALU.mult)
            I("dve", "tensor_tensor", [bkt[XY[1]]] + dent, [oT_t[0][m][n]], out=dst[64:128, :], in0=Y[64:128, :], in1=den[64:128, :],
              op=ALU.mult)

        def fox_attention():
            k = 0
            for m in range(4):
                for n in range(4):
                    fox_block(m, n, k % 2)
                    k += 1

        def sb_block(m, n, par):
            Bk = (3, 4)
            Ob = 5 + par
            for b in (Bk[0], Bk[1], Ob):
                I("pe", "matmul", [tok("zl"), tok("zr")], [bkt[b]], banks[b][:, :], lhsT=zl, rhs=zr, start=True, stop=False)
            nkb = 4 * (n + 1)
            tiles = []
            for kb in reversed(range(nkb)):
                for hl in range(2):
                    i = kb - 4 * n
                    tiles.append(dict(kb=kb, hl=hl, i=i, c0=max(i, 0) * 128, idx=len(tiles)))
            ntl = len(tiles)

            def st_z(T):
                b = nextbank()
                T["zb"] = b
                hl, kb, c0, i = T["hl"], T["kb"], T["c0"], T["i"]
                pl = slice(hl * 64, hl * 64 + 64)
                I("pe", "matmul", [A_t[4 + m][kb // 4], A_t[m][n]], [bkt[b]], banks[b][:, c0:512],
                  lhsT=A[pl, 4 + m, kb * 128:(kb + 1) * 128], rhs=A[pl, m, n * 512 + c0:(n + 1) * 512], start=True, stop=(i < 0))
                if i >= 0:
                    I("pe", "matmul", [tok("ident"), tok("nmS")], [bkt[b]], banks[b][:, c0:c0 + 128], lhsT=ident, rhs=nmS,
                      start=False, stop=True)

            def st_e(T):
                b, c0 = T["zb"], T["c0"]
                e_, et = dv(T["idx"] % 2, 1024, BF16)
                L_, Lt = dv((2, 3, 8)[T["idx"] % 3], 1024, BF16)
                T["e"], T["et"], T["L"], T["Lt"] = e_, et, L_, Lt
                I("act", "activation", [bkt[b]], et, out=e_[:, c0:512], in_=banks[b][:, c0:512], func=AF.Exp)
                I("act", "activation", et, Lt, out=L_[:, c0:512], in_=e_[:, c0:512], func=AF.Ln, bias=1.0)

            def st_ntri(T):
                hl, c0 = T["hl"], T["c0"]
                b = Bk[hl]
                I("pe", "matmul", [tok("ntri")] + T["Lt"], [bkt[b]], banks[b][:, c0:512], lhsT=ntri, rhs=T["L"][:, c0:512],
                  start=False, stop=False)

            def st_x(T):
                hl, c0 = T["hl"], T["c0"]
                b = Bk[hl]
                X_, Xt = dv(4 + T["idx"] % 2, 1024, BF16)
                T["X"], T["Xt"] = X_, Xt
                I("act", "activation", [bkt[b]], Xt, out=X_[:, c0:512], in_=banks[b][:, c0:512], func=AF.Exp)

            def st_tmo(T):
                hl, c0 = T["hl"], T["c0"]
                b = Bk[hl]
                I("pe", "matmul", [tok("tmo")] + T["Lt"], [bkt[b]], banks[b][:, c0:512], lhsT=tmo, rhs=T["L"][:, c0:512],
                  start=False, stop=False)

            def st_at(T):
                c0 = T["c0"]
                AT, ATt = dv(6 + T["idx"] % 2, 1024, BF16)
                T["AT"], T["ATt"] = AT, ATt
                I("dve", "tensor_tensor", T["et"] + T["Xt"], ATt, out=AT[:, c0:512], in0=T["e"][:, c0:512], in1=T["X"][:, c0:512],
                  op=ALU.mult)

            def st_pv(T):
                hl, kb, c0 = T["hl"], T["kb"], T["c0"]
                I("pe", "matmul", [V_t[kb]] + T["ATt"], [bkt[Ob]], banks[Ob][:, c0:512], lhsT=V[:, kb, m, hl * 64:hl * 64 + 128],
                  rhs=T["AT"][:, c0:512], start=False, stop=(T["idx"] == ntl - 1))

            run_pipe(tiles, [(0, st_z), (0, st_e), (1, st_ntri), (1, st_x), (2, st_tmo), (1, st_at), (2, st_pv)])
            I("dve", "tensor_copy", [bkt[Ob]], [oT_t[1][m][n]], out=oT[1][:, m, n * 512:(n + 1) * 512], in_=banks[Ob][:, :])

        def sb_attention():
            k = 0
            for m in range(4):
                for n in range(4):
                    sb_block(m, n, k % 2)
                    k += 1

        def gates_merge():
            for cg in range(2):
                Wga, gat, _ = wload(w_in[:, GA + cg * 512:GA + (cg + 1) * 512], 8, 512)
                Wgb, gbt, _ = wload(w_in[:, GB + cg * 512:GB + (cg + 1) * 512], 8, 512)
                Wbr, brt, si = wload(w_bf[:, cg * 512:(cg + 1) * 512], 4, 512)
                wload(w_bs[:, cg * 512:(cg + 1) * 512], 4, 512, into=si, kofs=4)
                for c in range(4):
                    cc = cg * 4 + c
                    cs = slice(c * 128, (c + 1) * 128)
                    for n in range(4):
                        ns = slice(n * 512, (n + 1) * 512)
                        sg = []
                        for j, (W, wt) in enumerate(((Wga, gat), (Wgb, gbt))):
                            b = nextbank()
                            mm_group(b, banks[b][:, :], [(W[:, kc, cs], hT[:, kc, ns]) for kc in range(8)], [wt] + hT_t[4 * n:4 * n + 4])
                            sgv, sgt = dv(4 + j, 2048, F32)
                            I("act", "activation", [bkt[b], tok("bgT")], sgt, out=sgv, in_=banks[b][:, :], func=AF.Sigmoid,
                              bias=bgT[:, j, cc:cc + 1])
                            sg.append((sgv, sgt))
                        tt = []
                        for j in range(2):
                            b = 3 + j
                            mm_group(b, banks[b][:, :], [(Wbr[:, 4 * j + kc, cs], oT[j][:, kc, ns]) for kc in range(4)],
                                     [brt] + [oT_t[j][kc][n] for kc in range(4)])
                            tv, tvt = dv(6 + j, 2048, F32)
                            I("dve", "tensor_tensor", [bkt[b]] + sg[j][1], tvt, out=tv, in0=banks[b][:, :], in1=sg[j][0], op=ALU.mult)
                            tt.append((tv, tvt))
                        I("dve", "tensor_tensor", tt[0][1] + tt[1][1], [A_t[cc][n]], out=A[:, cc, ns], in0=tt[0][0], in1=tt[1][0], op=ALU.add)

        def out_proj(seq):
            Wo = [wload(w_out[:, ch * 512:(ch + 1) * 512], 8, 512) for ch in range(2)]
            for tb in range(NB):
                xs, xst = dv(2 * (tb % 2), 4096, F32)
                r0 = seq * S + tb * 128
                I("sp", "dma_start", [], xst, out=xs, in_=x_d[r0:r0 + 128, :], dma=True, semkey="d_xs%d" % (tb % 2))
                for ch in range(2):
                    b = nextbank()
                    W, wt, _ = Wo[ch]
                    mm_group(b, banks[b][:, :], [(A[:, kc, tb * 128:(tb + 1) * 128], W[:, kc, :]) for kc in range(8)],
                             [wt] + [A_t[kc][tb // 4] for kc in range(8)])
                    wr = [x1_t[tb]] + (allB_old if (tb == 0 and ch == 0) else [])
                    I("dve", "tensor_tensor", [bkt[b]] + xst, wr, out=x1[:, tb, ch * 512:(ch + 1) * 512], in0=banks[b][:, :],
                      in1=xs[:, ch * 512:(ch + 1) * 512], op=ALU.add)

        def mlp():
            allA = [t for c in range(8) for t in A_t[c]]
            for g in range(8):
                Wu, wut, _ = wload(w_up[:, g * 512:(g + 1) * 512], 8, 512)
                Wd, wdt, _ = wload(w_down[g * 512:(g + 1) * 512, :], 4, 1024)
                for n in range(4):
                    hsel = (g * 4 + n) % 2
                    hd = hid[hsel]
                    hdt = [tok("hid%d_%d" % (hsel, c)) for c in range(4)]
                    for c in range(4):
                        b = nextbank()
                        mm_group(b, banks[b][:, :], [(Wu[:, kc, c * 128:(c + 1) * 128], hT[:, kc, n * 512:(n + 1) * 512]) for kc in range(8)],
                                 [wut] + hT_t[4 * n:4 * n + 4])
                        sq, sqt = dv(4 + (c % 2), 1024, BF16)
                        I("act", "activation", [bkt[b]], sqt, out=sq, in_=banks[b][:, :], func=AF.Square)
                        wr = [hdt[c]] + (allA if (g == 0 and n == 0 and c == 0) else [])
                        I("dve", "scalar_tensor_tensor", [bkt[b]] + sqt, wr, out=hd[:, c, :], in0=banks[b][:, :], scalar=0.0, in1=sq,
                          op0=ALU.is_gt, op1=ALU.mult)
                    for tl in range(4):
                        tb = 4 * n + tl
                        for ch in range(2):
                            b = nextbank2()
                            mm_group(b, banks[b][:, :], [(hd[:, kc, tl * 128:(tl + 1) * 128], Wd[:, kc, ch * 512:(ch + 1) * 512]) for kc in range(4)],
                                     [wdt] + hdt)
                            I("dve", "tensor_tensor", [bkt[b], x1_t[tb]], [x1_t[tb]], out=x1[:, tb, ch * 512:(ch + 1) * 512],
                              in0=banks[b][:, :], in1=x1[:, tb, ch * 512:(ch + 1) * 512], op=ALU.add)

        def ple(seq):
            allA = [t for c in range(8) for t in A_t[c]]
            pT_t = [tok("pT%d" % tb) for tb in range(NB)]
            for tb in range(NB):
                pin, pint = dv(4 + (tb % 2), 1024, F32)
                r0 = seq * S + tb * 128
                I("sp", "dma_start", [], pint, out=pin, in_=p_d[r0:r0 + 128, :], dma=True, semkey="d_pin%d" % (tb % 2))
                pbf, pbt = dv(6 + (tb % 2), 512, BF16)
                I("dve", "tensor_copy", pint, pbt, out=pbf, in_=pin)
                for c in range(2):
                    I("pe", "transpose", pbt + [tok("ident")], [bkt[7]], out=tpb[:, c, :], in_=pbf[:, c * 128:(c + 1) * 128], identity=ident)
                I("act", "activation", [bkt[7]], [pT_t[tb]] + (allA if tb == 0 else []), out=pT[:, :, tb * 128:(tb + 1) * 128],
                  in_=tpb[:, 0:2, :], func=AF.Copy)
            Wpg = [wload(w_pg[:, ch * 512:(ch + 1) * 512], 8, 512) for ch in range(2)]
            Wp, wpt, _ = wload(w_ple[:, :], 2, 1024)
            for tb in range(NB):
                for ch in range(2):
                    b = nextbank()
                    W, wt, _ = Wpg[ch]
                    mm_group(b, banks[b][:, :], [(hT[:, kc, tb * 128:(tb + 1) * 128], W[:, kc, :]) for kc in range(8)], [wt, hT_t[tb]])
                    sgv, sgt = dv(8 + (ch % 2), 2048, F32)
                    I("act", "activation", [bkt[b]], sgt, out=sgv, in_=banks[b][:, :], func=AF.Sigmoid)
                    b2 = nextbank2()
                    mm_group(b2, banks[b2][:, :], [(pT[:, kc, tb * 128:(tb + 1) * 128], Wp[:, kc, ch * 512:(ch + 1) * 512]) for kc in range(2)],
                             [wpt, pT_t[tb]])
                    I("dve", "tensor_tensor", [bkt[b2]] + sgt, sgt, out=sgv, in0=banks[b2][:, :], in1=sgv, op=ALU.mult)
                    I("dve", "tensor_tensor", sgt + [x1_t[tb]], [x1_t[tb]], out=x1[:, tb, ch * 512:(ch + 1) * 512], in0=sgv,
                      in1=x1[:, tb, ch * 512:(ch + 1) * 512], op=ALU.add)

        store_t = []

        def final_norm(seq):
            load_g(g_final)
            for tb in range(NB):
                xin = x1[:, tb, :]
                rms_stats(xin, [x1_t[tb]], tb)
                ob, obt = dv(2 * (tb % 2), 4096, F32)
                I("dve", "scalar_tensor_tensor", [x1_t[tb], tok("rstd%d" % tb), tok("gt")], obt, out=ob, in0=xin, scalar=rstd[:, tb:tb + 1],
                  in1=gt, op0=ALU.mult, op1=ALU.mult)
                r0 = seq * S + tb * 128
                st = tok("store%d_%d" % (seq, tb))
                store_t.append(st)
                I("sp", "dma_start", obt, [st], out=out_d[r0:r0 + 128, :], in_=ob, dma=True, semkey="d_store%d" % (tb % 2))

        for seq in range(NSEQ):
            norm_T(seq, True, g_mix)
            in_proj(0)
            forget_tables()
            fox_attention()
            in_proj(1)
            sb_attention()
            gates_merge()
            out_proj(seq)
            norm_T(seq, False, g_mlp)
            mlp()
            norm_T(seq, False, g_ple)
            ple(seq)
            final_norm(seq)
        P.add("sp", None, reads=store_t)

        counters = P.assign()
        semh = {k: es.enter_context(nc.semaphore("s_" + k)) for k in counters}
        with nc.Block() as block:
            @block.tensor
            def _(e):
                P.emit_engine("pe", e, semh)

            @block.scalar
            def _(e):
                P.emit_engine("act", e, semh)

            @block.vector
            def _(e):
                P.emit_engine("dve", e, semh)

            @block.gpsimd
            def _(e):
                P.emit_engine("pool", e, semh)

            @block.sync
            def _(e):
                P.emit_engine("sp", e, semh)
    return nc


def kernel(x, p, g_mix, w_in, b_forget, b_gate, w_branch_fox, w_branch_sb, w_out,
           g_mlp, w_up, w_down, g_ple, w_ple_gate, w_ple, g_final):
    f = lambda a: np.ascontiguousarray(np.asarray(a, dtype=np.float32))
    x = f(x)
    p = f(p)
    common = {
        "w_in": f(w_in)[0], "b_forget": f(b_forget).reshape(1, 8), "b_gate": f(b_gate)[0],
        "w_bf": f(w_branch_fox)[0], "w_bs": f(w_branch_sb)[0], "w_out": f(w_out)[0],
        "g_mix": f(g_mix).reshape(1, DM), "g_mlp": f(g_mlp).reshape(1, DM), "g_ple": f(g_ple).reshape(1, DM),
        "g_final": f(g_final).reshape(1, DM), "w_up": f(w_up)[0], "w_down": f(w_down)[0],
        "w_pg": f(w_ple_gate)[0], "w_ple": f(w_ple)[0],
    }
    in_maps = []
    for c in range(NCORES):
        m = dict(common)
        m["x"] = x[c * NSEQ:(c + 1) * NSEQ].reshape(NSEQ * S, DM)
        m["p"] = p[0, c * NSEQ:(c + 1) * NSEQ].reshape(NSEQ * S, 256)
        in_maps.append(m)
    nc = build_program()
    res = run_bass_kernel_spmd(nc, in_maps, core_ids=list(range(NCORES)))
    out = np.stack([np.asarray(r["out"]).reshape(NSEQ, S, DM) for r in res.results], axis=0)
    return out.reshape(NCORES * NSEQ, S, DM).astype(np.float32)
```

```python
from contextlib import ExitStack
import numpy as np
import concourse.bass as bass
import concourse.mybir as mybir
from concourse.bass_utils import run_bass_kernel_spmd
from concourse.ap import AP

F32 = mybir.dt.float32
BF16 = mybir.dt.bfloat16
U8 = mybir.dt.uint8
AF = mybir.ActivationFunctionType
ALU = mybir.AluOpType

NCORES = 8
S = 2048
DM = 1024
NB = 16
NSEQ = 2
EPS = 1e-6
QF, KF, VF, FF, QS, KS, VS, GA, GB = 0, 512, 1024, 1536, 1544, 2056, 2568, 3080, 4104
NEG = -30000.0
import os
NOSTRICT = bool(int(os.environ.get("MK_NOSTRICT", "0")))
NOPRUNE = bool(int(os.environ.get("MK_NOPRUNE", "0")))


class Buf:
    __slots__ = ("name", "last_w", "readers")

    def __init__(self, name):
        self.name = name
        self.last_w = None
        self.readers = []


class Op:
    __slots__ = ("eng", "fn", "deps", "signal", "semkey", "semval", "dma", "idx", "waits", "clock")


class Prog:
    ENGS = ("pe", "act", "dve", "pool", "sp")

    def __init__(self):
        self.ops = {e: [] for e in self.ENGS}
        self.n = 0

    def add(self, eng, fn, reads=(), writes=(), dma=False, semkey=None):
        op = Op()
        op.eng, op.fn, op.dma, op.idx = eng, fn, dma, self.n
        self.n += 1
        op.signal = False
        op.semval = None
        if dma:
            op.semkey = semkey if semkey is not None else "d_" + (writes[0].name if writes else reads[0].name)
        else:
            op.semkey = eng
        deps = {}
        for b in reads:
            d = b.last_w
            if d is not None:
                deps[d.idx] = (d, True)
        for b in writes:
            d = b.last_w
            if d is not None and d.idx not in deps:
                deps[d.idx] = (d, False)
            for r in b.readers:
                if r.idx not in deps:
                    deps[r.idx] = (r, False)
        keep = []
        for d, raw in deps.values():
            if d.dma or dma or d.eng != eng:
                keep.append(d)
            elif eng == "pe":
                continue
            elif raw or not NOSTRICT:
                keep.append(d)
        for d in keep:
            d.signal = True
        op.deps = keep
        for b in reads:
            if not dma:
                b.readers = [r for r in b.readers if r.dma or r.eng != eng]
            b.readers.append(op)
        for b in writes:
            b.last_w = op
            b.readers = []
        self.ops[eng].append(op)
        return op

    def ins(self, eng, meth, reads, writes, *args, dma=False, semkey=None, **kwargs):
        def fn(e, meth=meth, args=args, kwargs=kwargs):
            return getattr(e, meth)(*args, **kwargs)
        return self.add(eng, fn, reads=reads, writes=writes, dma=dma, semkey=semkey)

    def assign(self):
        counters = {}
        allops = sorted((op for e in self.ENGS for op in self.ops[e]), key=lambda o: o.idx)
        for op in allops:
            if op.signal:
                inc = 16 if op.dma else 1
                counters[op.semkey] = counters.get(op.semkey, 0) + inc
                op.semval = counters[op.semkey]
        known = {e: {} for e in self.ENGS}
        for op in allops:
            K = known[op.eng]
            waits = []
            for d in sorted(op.deps, key=lambda o: -o.idx):
                if K.get(d.semkey, 0) < d.semval:
                    waits.append((d.semkey, d.semval))
                    K[d.semkey] = d.semval
                if not NOPRUNE:
                    for k, v in d.clock.items():
                        if K.get(k, 0) < v:
                            K[k] = v
            best = {}
            for k, v in waits:
                if best.get(k, 0) < v:
                    best[k] = v
            op.waits = list(best.items())
            if op.signal:
                op.clock = dict(K)
                op.clock[op.semkey] = op.semval
            else:
                op.clock = None
        return counters

    def emit_engine(self, eng_name, eng, semh):
        for op in self.ops[eng_name]:
            todo = list(op.waits)
            attach = todo.pop() if (todo and op.fn is not None) else None
            for k, v in todo:
                eng.wait_ge(semh[k], v)
            ins = op.fn(eng) if op.fn is not None else None
            if attach is not None:
                ins._wait_ge(semh[attach[0]], attach[1])
            if op.signal:
                if ins is None:
                    ins = eng.nop()
                ins.then_inc(semh[op.semkey], 16 if op.dma else 1)


def build_program():
    nc = bass.Bass("TRN2", target_bir_lowering=False)
    dt_in = lambda n, shp: nc.dram_tensor(n, shp, F32, kind="ExternalInput").ap()
    x_d = dt_in("x", [NSEQ * S, DM])
    p_d = dt_in("p", [NSEQ * S, 256])
    w_in = dt_in("w_in", [DM, 5128])
    b_forget = dt_in("b_forget", [1, 8])
    b_gate = dt_in("b_gate", [2, DM])
    w_bf = dt_in("w_bf", [512, DM])
    w_bs = dt_in("w_bs", [512, DM])
    w_out = dt_in("w_out", [DM, DM])
    g_mix = dt_in("g_mix", [1, DM])
    g_mlp = dt_in("g_mlp", [1, DM])
    g_ple = dt_in("g_ple", [1, DM])
    g_final = dt_in("g_final", [1, DM])
    w_up = dt_in("w_up", [DM, 4096])
    w_down = dt_in("w_down", [4096, DM])
    w_pg = dt_in("w_pg", [DM, DM])
    w_ple = dt_in("w_ple", [256, DM])
    out_d = nc.dram_tensor("out", [NSEQ * S, DM], F32, kind="ExternalOutput").ap()

    DBG = bool(int(os.environ.get("MK_DEBUG", "0")))
    if DBG:
        dbg = {
            "hT": nc.dram_tensor("dbg_hT", [128, 8, 2048], BF16, kind="ExternalOutput").ap(),
            "A": nc.dram_tensor("dbg_A", [128, 8, 2048], BF16, kind="ExternalOutput").ap(),
            "V": nc.dram_tensor("dbg_V", [128, 16, 4, 192], BF16, kind="ExternalOutput").ap(),
            "oTf": nc.dram_tensor("dbg_oTf", [128, 4, 2048], BF16, kind="ExternalOutput").ap(),
            "oTs": nc.dram_tensor("dbg_oTs", [128, 4, 2048], BF16, kind="ExternalOutput").ap(),
            "x1a": nc.dram_tensor("dbg_x1a", [128, 16, 1024], F32, kind="ExternalOutput").ap(),
            "x1b": nc.dram_tensor("dbg_x1b", [128, 16, 1024], F32, kind="ExternalOutput").ap(),
        }
    P = Prog()
    es = ExitStack()
    with es:
        ARENA = 205 * 1024
        arena = es.enter_context(nc.sbuf_tensor("arena", [128, ARENA], U8))
        cur = [0]

        def carve(nbytes):
            off = cur[0]
            cur[0] += (nbytes + 63) // 64 * 64
            assert cur[0] <= ARENA, cur[0]
            return off

        def view(off, nbytes, dt, pat=None, **kw):
            v = arena[:, off:off + nbytes].bitcast(dt)
            if pat is not None:
                v = v.rearrange(pat, **kw)
            return v

        def tile(nbytes, dt, pat=None, **kw):
            return view(carve(nbytes), nbytes, dt, pat, **kw)

        ident = tile(256, BF16)
        identf = tile(512, F32)
        ntri = tile(256, BF16)
        tmo = tile(256, BF16)
        nmF = tile(256, BF16)
        nmS = tile(256, BF16)
        zl = tile(256, BF16)
        zr = tile(1024, BF16)
        onesb = tile(256, BF16)
        negb = tile(256, BF16)
        ntriincf = tile(512, F32)
        negonesf = tile(512, F32)
        negf = tile(512, F32)
        bfb = tile(512, F32)
        bgT = tile(64, F32, "p (j c) -> p j c", j=2)
        gt = tile(4096, F32)
        wf = tile(128, BF16, "p (k c) -> p k c", k=8)
        ss = tile(64, F32)
        fstat = tile(64, F32)
        lnv = tile(64, F32)
        rstd = tile(64, F32)
        cfull = tile(512, F32, "p (t h) -> p t h", t=16)
        cpre = tile(512, F32, "p (t h) -> p t h", t=16)
        totsb = tile(512, F32, "p (t h) -> p t h", t=16)
        tF = tile(512, F32)
        eF = tile(512, F32)
        spF = tile(512, F32)
        B = {}

        def tok(n):
            if n not in B:
                B[n] = Buf(n)
            return B[n]

        NSLOT = 5
        slots = [tile(8192, BF16) for _ in range(NSLOT)]
        hT = tile(32768, BF16, "p (c t) -> p c t", c=8)
        Aoff = carve(32768)
        A = view(Aoff, 32768, BF16, "p (c t) -> p c t", c=8)
        pT = view(Aoff, 8192, BF16, "p (c t) -> p c t", c=2)
        hid = [view(Aoff + 8192 + i * 4096, 4096, BF16, "p (c t) -> p c t", c=4) for i in range(2)]
        Boff = carve(65536)
        V = view(Boff, 24576, BF16, "p (t m c) -> p t m c", t=16, m=4)
        oT = [view(Boff + 24576 + i * 16384, 16384, BF16, "p (c t) -> p c t", c=4) for i in range(2)]
        x1 = view(Boff, 65536, F32, "p (t c) -> p t c", t=16)
        ND = 10
        Doff = carve(ND * 2048)
        Dh = [tok("Dh%d" % i) for i in range(2 * ND)]

        def dv(chunk, nbytes, dt, pat=None, **kw):
            nch = (nbytes + 2047) // 2048
            assert chunk + nch <= ND
            return view(Doff + chunk * 2048, nbytes, dt, pat, **kw), Dh[2 * chunk:2 * (chunk + nch)]

        def dvh(half, nbytes, dt, pat=None, **kw):
            nh = (nbytes + 1023) // 1024
            assert half + nh <= 2 * ND
            return view(Doff + half * 1024, nbytes, dt, pat, **kw), Dh[half:half + nh]

        banks = [es.enter_context(nc.psum_tensor("bank%d" % i, [128, 512], F32)) for i in range(8)]
        bkt = [tok("bank%d" % i) for i in range(8)]
        tpb = banks[7][:, :].bitcast(BF16).rearrange("p (c t) -> p c t", c=8)
        rot = {"mm": 0}

        hT_t = [tok("hT%d" % i) for i in range(NB)]
        A_t = [[tok("A%d_%d" % (c, n)) for n in range(4)] for c in range(8)]
        V_t = [tok("V%d" % i) for i in range(NB)]
        oT_t = [[[tok("oT%d_%d_%d" % (b, m, n)) for n in range(4)] for m in range(4)] for b in range(2)]
        x1_t = [tok("x1_%d" % i) for i in range(NB)]
        slot_t = [tok("slot%d" % i) for i in range(NSLOT)]
        allB_old = V_t + [t for b in range(2) for m in range(4) for t in oT_t[b][m]]
        slot_i = [0]

        I = P.ins

        def wload(src2d, kc, cols, into=None, kofs=0):
            if into is None:
                si = slot_i[0] % NSLOT
                slot_i[0] += 1
            else:
                si = into
            kct = 4096 // cols
            vw = slots[si].rearrange("p (k c) -> p k c", k=kct)
            I("pool", "dma_start", [], [slot_t[si]], out=vw[:, kofs:kofs + kc, :],
              in_=src2d.rearrange("(k p) c -> p k c", p=128), dma=True)
            return vw, slot_t[si], si

        def mm_group(bank_i, out_ap, pairs, reads):
            n = len(pairs)
            for i, (l, r) in enumerate(pairs):
                I("pe", "matmul", reads, [bkt[bank_i]], out_ap, lhsT=l, rhs=r, start=(i == 0), stop=(i == n - 1))

        def nextbank(nrot=4):
            b = rot["mm"] % nrot
            rot["mm"] += 1
            return b

        def nextbank2():
            b = 4 + rot["mm2"] % 4
            rot["mm2"] += 1
            return b
        rot["mm2"] = 0

        I("dve", "memset", [], [tok("onesb")], onesb, 1.0)
        I("dve", "memset", [], [tok("negb")], negb, -1.0)
        I("dve", "memset", [], [tok("zl")], zl, 0.0)
        I("dve", "memset", [], [tok("zr")], zr, 0.0)
        I("dve", "memset", [], [tok("negf")], negf, -1.0)
        I("dve", "memset", [], [tok("negonesf")], negonesf, -1.0)

        def asel(out, in_, pat, cm, op, fill, rd, wr):
            I("pool", "affine_select", [tok(rd)], [tok(wr)], out=out, in_=in_, pattern=pat, compare_op=op, fill=fill, base=0,
              channel_multiplier=cm)
        asel(ident, onesb, [[-1, 128]], 1, ALU.is_equal, 0.0, "onesb", "ident")
        asel(ntri, negb, [[-1, 128]], 1, ALU.is_ge, 0.0, "negb", "ntri")
        asel(tmo, negb, [[1, 128]], -1, ALU.is_gt, 0.0, "negb", "tmo")
        asel(nmF, zl, [[1, 128]], -1, ALU.is_ge, NEG, "zl", "nmF")
        asel(nmS, zl, [[1, 128]], -1, ALU.is_gt, NEG, "zl", "nmS")
        asel(ntriincf, negf, [[1, 128]], -1, ALU.is_ge, 0.0, "negf", "ntriincf")
        I("dve", "tensor_copy", [tok("ident")], [tok("identf")], out=identf, in_=ident)
        bf8 = tile(64, F32)
        I("sp", "dma_start", [], [tok("bf8")], out=bf8[:, 0:8], in_=AP(b_forget.tensor, 0, [[0, 128], [1, 8]]), dma=True)
        bf8b = AP(bf8.tensor, bf8.offset, [list(bf8.ap[0]), [0, 16], [1, 8]])
        I("dve", "tensor_copy", [tok("bf8")], [tok("bfb")], out=bfb.rearrange("p (t h) -> p t h", t=16), in_=bf8b)
        I("sp", "dma_start", [], [tok("bgT")], out=bgT, in_=b_gate.rearrange("j (c p) -> p j c", p=128),
          allow_slow_non_contiguous=True, dma=True)

        def load_g(g_ap):
            I("sp", "dma_start", [], [tok("gt")], out=gt, in_=AP(g_ap.tensor, 0, [[0, 128], [1, DM]]), dma=True)

        def rms_stats(xin, xr, tb):
            junk, jt = dv(8, 2048, BF16)
            sst = tok("ss%d" % tb)
            I("act", "activation", xr, jt + [sst], out=junk, in_=xin, func=AF.Square, accum_out=ss[:, tb:tb + 1])
            I("act", "activation", [sst], [tok("lnv%d" % tb)], out=lnv[:, tb:tb + 1], in_=ss[:, tb:tb + 1], func=AF.Ln,
              scale=1.0 / DM, bias=EPS)
            I("act", "activation", [tok("lnv%d" % tb)], [tok("rstd%d" % tb)], out=rstd[:, tb:tb + 1], in_=lnv[:, tb:tb + 1],
              func=AF.Exp, scale=-0.5)

        def norm_A(seq, tb, from_dram):
            if from_dram:
                xin, xr = dv(4 + 2 * (tb % 2), 4096, F32)
                r0 = seq * S + tb * 128
                I("sp", "dma_start", [], xr, out=xin, in_=x_d[r0:r0 + 128, :], dma=True, semkey="d_xin%d" % (tb % 2))
            else:
                xin, xr = x1[:, tb, :], [x1_t[tb]]
            rms_stats(xin, xr, tb)
            hbf, ht = dv(tb % 2, 2048, BF16)
            I("dve", "scalar_tensor_tensor", xr + [tok("rstd%d" % tb), tok("gt")], ht, out=hbf, in0=xin, scalar=rstd[:, tb:tb + 1],
              in1=gt, op0=ALU.mult, op1=ALU.mult)

        def norm_B(tb):
            hbf, ht = dv(tb % 2, 2048, BF16)
            for c in range(8):
                I("pe", "transpose", ht + [tok("ident")], [bkt[7]], out=tpb[:, c, :], in_=hbf[:, c * 128:(c + 1) * 128], identity=ident)
            if tb % 2 == 0:
                I("act", "activation", [bkt[7]], [hT_t[tb]], out=hT[:, :, tb * 128:(tb + 1) * 128], in_=tpb, func=AF.Copy)
            else:
                I("dve", "tensor_copy", [bkt[7]], [hT_t[tb]], out=hT[:, :, tb * 128:(tb + 1) * 128], in_=tpb)

        class NormStream:
            def __init__(self, seq, from_dram, g_ap):
                load_g(g_ap)
                self.seq, self.fd, self.a, self.b = seq, from_dram, 0, 0

            def upto(self, nb):
                nb = min(nb, NB)
                while self.b < nb:
                    while self.a < min(self.b + 2, NB):
                        norm_A(self.seq, self.a, self.fd)
                        self.a += 1
                    norm_B(self.b)
                    self.b += 1
                    if self.a < NB and self.a < self.b + 2:
                        norm_A(self.seq, self.a, self.fd)
                        self.a += 1

        def in_proj(br, ns=None):
            qo, ko, vo = (QF, KF, VF) if br == 0 else (QS, KS, VS)
            Wq, wqt, _ = wload(w_in[:, qo:qo + 512], 8, 512)
            Wk, wkt, _ = wload(w_in[:, ko:ko + 512], 8, 512)
            Wv, wvt, _ = wload(w_in[:, vo:vo + 512], 8, 512)
            if br == 0:
                I("pool", "dma_start", [], [tok("wf")], out=wf, in_=w_in[:, FF:FF + 8].rearrange("(k p) c -> p k c", p=128), dma=True)
            I("dve", "memset", [], V_t + (x1_t if br == 0 else []), V[:, :, :, 64:128], 1.0 if br == 0 else 0.0)
            for n in range(4):
                if ns is not None:
                    ns.upto(4 * n + 4)
                for which, W, wt in ((0, Wq, wqt), (1, Wk, wkt)):
                    for m in range(4):
                        b = nextbank(4)
                        mm_group(b, banks[b][:, :], [(W[:, kc, m * 128:(m + 1) * 128], hT[:, kc, n * 512:(n + 1) * 512]) for kc in range(8)],
                                 [wt] + hT_t[4 * n:4 * n + 4])
                        dst = A[:, which * 4 + m, n * 512:(n + 1) * 512]
                        if which == 0:
                            I("act", "activation", [bkt[b]], [A_t[m][n]], out=dst, in_=banks[b][:, :], func=AF.Copy, scale=0.125)
                        else:
                            I("dve", "tensor_copy", [bkt[b]], [A_t[4 + m][n]], out=dst, in_=banks[b][:, :])
                for tb in range(4 * n, 4 * n + 4):
                    b = nextbank(4)
                    mm_group(b, banks[b][:, :], [(hT[:, kc, tb * 128:(tb + 1) * 128], Wv[:, kc, :]) for kc in range(8)], [wvt, hT_t[tb]])
                    dst = V[:, tb, :, :].rearrange("p m (a c) -> p m a c", a=3)[:, :, 0:3:2, :]
                    src = banks[b][:, :].rearrange("p (m a c) -> p m a c", m=4, a=2)
                    if tb % 2 == 0:
                        I("act", "activation", [bkt[b]], [V_t[tb]], out=dst, in_=src, func=AF.Copy)
                    else:
                        I("dve", "tensor_copy", [bkt[b]], [V_t[tb]], out=dst, in_=src)

        def forget_tables():
            fb = 7
            for tb in range(NB):
                mm_group(fb, banks[fb][:, tb * 8:(tb + 1) * 8], [(hT[:, kc, tb * 128:(tb + 1) * 128], wf[:, kc, :]) for kc in range(8)],
                         [tok("wf"), hT_t[tb]])
            I("dve", "tensor_tensor", [bkt[fb], tok("bfb")], [tok("tF")], out=tF, in0=banks[fb][:, 0:128], in1=bfb, op=ALU.add)
            I("act", "activation", [tok("tF")], [tok("eF")], out=eF, in_=tF, func=AF.Exp, scale=-1.0)
            I("act", "activation", [tok("eF")], [tok("spF")], out=spF, in_=eF, func=AF.Ln, bias=1.0)
            I("pe", "matmul", [tok("spF"), tok("ntriincf")], [bkt[fb]], banks[fb][:, 128:256], lhsT=ntriincf, rhs=spF, start=True, stop=True)
            I("pe", "matmul", [tok("spF"), tok("negonesf")], [bkt[fb]], banks[fb][:, 256:384], lhsT=negonesf, rhs=spF, start=True, stop=True)
            I("dve", "tensor_copy", [bkt[fb]], [tok("totsb")], out=totsb, in_=banks[fb][:, 256:384].rearrange("p (t h) -> p t h", t=16))
            I("dve", "memset", [], [tok("cpre")], cpre[:, 0, :], 0.0)
            for tb in range(1, NB):
                I("dve", "tensor_tensor", [tok("cpre"), tok("totsb")], [tok("cpre")], out=cpre[:, tb, :], in0=cpre[:, tb - 1, :],
                  in1=totsb[:, tb - 1, :], op=ALU.add)
            I("dve", "tensor_tensor", [bkt[fb], tok("cpre")], [tok("cfull")], out=cfull,
              in0=banks[fb][:, 128:256].rearrange("p (t h) -> p t h", t=16), in1=cpre, op=ALU.add)

        def run_pipe(tiles, stages):
            maxd = max(d for d, _ in stages)
            for t in range(len(tiles) + maxd):
                for d, fn in stages:
                    j = t - d
                    if 0 <= j < len(tiles):
                        fn(tiles[j])

        def fox_block(m, n, par):
            XY = (4 + 2 * par, 5 + 2 * par)
            cts = []
            for hl in range(2):
                h = 2 * m + hl
                ct, ctt = dv(hl + 2 * par, 2048, F32)
                cb = nextbank(4)
                for j in range(4):
                    qb = 4 * n + j
                    I("pe", "matmul", [tok("cfull"), tok("identf")], [bkt[cb]], banks[cb][:, j * 128:(j + 1) * 128],
                      lhsT=cfull[:, qb, h:h + 1].to_broadcast([128, 128]), rhs=identf, start=True, stop=True)
                I("act", "activation", [bkt[cb]], ctt, out=ct, in_=banks[cb][:, :], func=AF.Copy)
                cts.append((ct, ctt))
            nkb = 4 * (n + 1)
            tiles = []
            for kb in range(nkb):
                i = kb - 4 * n
                tiles.append(dict(kb=kb, i=i, c0=max(i, 0) * 128, idx=len(tiles)))

            def st_z(T):
                kb, c0, i = T["kb"], T["c0"], T["i"]
                T["zb"] = [nextbank(4), nextbank(4)]
                for hl in range(2):
                    b = T["zb"][hl]
                    pl = slice(hl * 64, hl * 64 + 64)
                    I("pe", "matmul", [A_t[4 + m][kb // 4], A_t[m][n]], [bkt[b]], banks[b][:, c0:512],
                      lhsT=A[pl, 4 + m, kb * 128:(kb + 1) * 128], rhs=A[pl, m, n * 512 + c0:(n + 1) * 512], start=True, stop=(i < 0))
                if i >= 0:
                    for hl in range(2):
                        b = T["zb"][hl]
                        I("pe", "matmul", [tok("ident"), tok("nmF")], [bkt[b]], banks[b][:, c0:c0 + 128], lhsT=ident, rhs=nmF,
                          start=False, stop=True)

            def st_lg(T):
                kb, c0 = T["kb"], T["c0"]
                T["lg"] = []
                for hl in range(2):
                    b = T["zb"][hl]
                    h = 2 * m + hl
                    lg, lgt = dv(4 + (2 * T["idx"] + hl) % 3, 2048, F32)
                    T["lg"].append((lg, lgt))
                    ct, ctt = cts[hl]
                    I("dve", "scalar_tensor_tensor", [bkt[b], tok("cfull")] + ctt, lgt, out=lg[:, c0:512], in0=banks[b][:, c0:512],
                      scalar=cfull[:, kb, h:h + 1], in1=ct[:, c0:512], op0=ALU.subtract, op1=ALU.add)

            def st_exp(T):
                c0 = T["c0"]
                T["pt"] = []
                for hl in range(2):
                    pt, ptt = dvh(14 + (2 * T["idx"] + hl) % 6, 1024, BF16)
                    T["pt"].append((pt, ptt))
                    lg, lgt = T["lg"][hl]
                    I("act", "activation", lgt, ptt, out=pt[:, c0:512], in_=lg[:, c0:512], func=AF.Exp)

            def st_pv(T):
                kb, c0 = T["kb"], T["c0"]
                for hl in range(2):
                    b = XY[hl]
                    pt, ptt = T["pt"][hl]
                    I("pe", "matmul", [V_t[kb]] + ptt, [bkt[b]], banks[b][:, c0:512], lhsT=V[:, kb, m, hl * 64:hl * 64 + 128],
                      rhs=pt[:, c0:512], start=(kb == 0), stop=(kb == nkb - 1))

            run_pipe(tiles, [(0, st_z), (0, st_lg), (0, st_exp), (2, st_pv)])
            X, Y = banks[XY[0]], banks[XY[1]]
            den, dent = dv(4, 2048, F32)
            rec, rect = dv(5, 2048, F32)
            I("dve", "tensor_copy", [bkt[XY[0]]], dent, out=den[0:64, :], in_=X[64:128, :])
            I("dve", "tensor_copy", [bkt[XY[1]]], dent, out=den[64:128, :], in_=Y[0:64, :])
            I("act", "activation", dent, rect, out=rec, in_=den, func=AF.Ln)
            I("act", "activation", rect, dent, out=den, in_=rec, func=AF.Exp, scale=-1.0)
            dst = oT[0][:, m, n * 512:(n + 1) * 512]
            I("dve", "tensor_tensor", [bkt[XY[0]]] + dent, [oT_t[0][m][n]], out=dst[0:64, :], in0=X[0:64, :], in1=den[0:64, :], op=ALU.mult)
            I("dve", "tensor_tensor", [bkt[XY[1]]] + dent, [oT_t[0][m][n]], out=dst[64:128, :], in0=Y[64:128, :], in1=den[64:128, :],
              op=ALU.mult)

        def fox_attention():
            k = 0
            for m in range(4):
                for n in range(4):
                    fox_block(m, n, k % 2)
                    k += 1

        def sb_block(m, n, par):
            Bk = (4, 5)
            Ob = 6 + par
            for b in (Bk[0], Bk[1], Ob):
                I("pe", "matmul", [tok("zl"), tok("zr")], [bkt[b]], banks[b][:, :], lhsT=zl, rhs=zr, start=True, stop=False)
            nkb = 4 * (n + 1)
            tiles = []
            for kb in reversed(range(nkb)):
                i = kb - 4 * n
                tiles.append(dict(kb=kb, i=i, c0=max(i, 0) * 128, idx=len(tiles)))
            ntl = len(tiles)

            def st_z(T):
                kb, c0, i = T["kb"], T["c0"], T["i"]
                T["zb"] = [nextbank(4), nextbank(4)]
                for hl in range(2):
                    b = T["zb"][hl]
                    pl = slice(hl * 64, hl * 64 + 64)
                    I("pe", "matmul", [A_t[4 + m][kb // 4], A_t[m][n]], [bkt[b]], banks[b][:, c0:512],
                      lhsT=A[pl, 4 + m, kb * 128:(kb + 1) * 128], rhs=A[pl, m, n * 512 + c0:(n + 1) * 512], start=True, stop=(i < 0))
                if i >= 0:
                    for hl in range(2):
                        b = T["zb"][hl]
                        I("pe", "matmul", [tok("ident"), tok("nmS")], [bkt[b]], banks[b][:, c0:c0 + 128], lhsT=ident, rhs=nmS,
                          start=False, stop=True)

            def st_e(T):
                c0 = T["c0"]
                T["e"], T["L"] = [], []
                for hl in range(2):
                    b = T["zb"][hl]
                    e_, et = dvh((2 * T["idx"] + hl) % 4, 1024, BF16)
                    T["e"].append((e_, et))
                    I("act", "activation", [bkt[b]], et, out=e_[:, c0:512], in_=banks[b][:, c0:512], func=AF.Exp)
                for hl in range(2):
                    e_, et = T["e"][hl]
                    L_, Lt = dvh(4 + (2 * T["idx"] + hl) % 6, 1024, BF16)
                    T["L"].append((L_, Lt))
                    I("act", "activation", et, Lt, out=L_[:, c0:512], in_=e_[:, c0:512], func=AF.Ln, bias=1.0)

            def st_ntri(T):
                c0 = T["c0"]
                for hl in range(2):
                    L_, Lt = T["L"][hl]
                    I("pe", "matmul", [tok("ntri")] + Lt, [bkt[Bk[hl]]], banks[Bk[hl]][:, c0:512], lhsT=ntri, rhs=L_[:, c0:512],
                      start=False, stop=False)

            def st_x(T):
                c0 = T["c0"]
                T["X"] = []
                for hl in range(2):
                    X_, Xt = dvh(10 + (2 * T["idx"] + hl) % 4, 1024, BF16)
                    T["X"].append((X_, Xt))
                    I("act", "activation", [bkt[Bk[hl]]], Xt, out=X_[:, c0:512], in_=banks[Bk[hl]][:, c0:512], func=AF.Exp)

            def st_tmo(T):
                c0 = T["c0"]
                for hl in range(2):
                    L_, Lt = T["L"][hl]
                    I("pe", "matmul", [tok("tmo")] + Lt, [bkt[Bk[hl]]], banks[Bk[hl]][:, c0:512], lhsT=tmo, rhs=L_[:, c0:512],
                      start=False, stop=False)

            def st_at(T):
                c0 = T["c0"]
                T["AT"] = []
                for hl in range(2):
                    AT, ATt = dvh(14 + (2 * T["idx"] + hl) % 4, 1024, BF16)
                    T["AT"].append((AT, ATt))
                    e_, et = T["e"][hl]
                    X_, Xt = T["X"][hl]
                    I("dve", "tensor_tensor", et + Xt, ATt, out=AT[:, c0:512], in0=e_[:, c0:512], in1=X_[:, c0:512], op=ALU.mult)

            def st_pv(T):
                kb, c0 = T["kb"], T["c0"]
                for hl in range(2):
                    AT, ATt = T["AT"][hl]
                    I("pe", "matmul", [V_t[kb]] + ATt, [bkt[Ob]], banks[Ob][:, c0:512], lhsT=V[:, kb, m, hl * 64:hl * 64 + 128],
                      rhs=AT[:, c0:512], start=False, stop=(T["idx"] == ntl - 1 and hl == 1))

            run_pipe(tiles, [(0, st_z), (0, st_e), (2, st_tmo), (1, st_ntri), (1, st_x), (1, st_at), (2, st_pv)])
            I("dve", "tensor_copy", [bkt[Ob]], [oT_t[1][m][n]], out=oT[1][:, m, n * 512:(n + 1) * 512], in_=banks[Ob][:, :])

        def sb_attention():
            k = 0
            for m in range(4):
                for n in range(4):
                    sb_block(m, n, k % 2)
                    k += 1

        def gates_merge():
            for cg in range(2):
                Wga, gat, _ = wload(w_in[:, GA + cg * 512:GA + (cg + 1) * 512], 8, 512)
                Wgb, gbt, _ = wload(w_in[:, GB + cg * 512:GB + (cg + 1) * 512], 8, 512)
                Wbr, brt, si = wload(w_bf[:, cg * 512:(cg + 1) * 512], 4, 512)
                wload(w_bs[:, cg * 512:(cg + 1) * 512], 4, 512, into=si, kofs=4)
                for c in range(4):
                    cc = cg * 4 + c
                    cs = slice(c * 128, (c + 1) * 128)
                    for n in range(4):
                        ns = slice(n * 512, (n + 1) * 512)
                        it = c * 4 + n
                        sg = []
                        for j, (W, wt) in enumerate(((Wga, gat), (Wgb, gbt))):
                            b = nextbank(4)
                            mm_group(b, banks[b][:, :], [(W[:, kc, cs], hT[:, kc, ns]) for kc in range(8)], [wt] + hT_t[4 * n:4 * n + 4])
                            sgv, sgt = dv(2 * (it % 2) + j, 2048, F32)
                            I("act", "activation", [bkt[b], tok("bgT")], sgt, out=sgv, in_=banks[b][:, :], func=AF.Sigmoid,
                              bias=bgT[:, j, cc:cc + 1])
                            sg.append((sgv, sgt))
                        tt = []
                        for j in range(2):
                            b = 4 + 2 * (it % 2) + j
                            mm_group(b, banks[b][:, :], [(Wbr[:, 4 * j + kc, cs], oT[j][:, kc, ns]) for kc in range(4)],
                                     [brt] + [oT_t[j][kc][n] for kc in range(4)])
                            tv, tvt = dv(4 + 2 * (it % 2) + j, 2048, F32)
                            I("dve", "tensor_tensor", [bkt[b]] + sg[j][1], tvt, out=tv, in0=banks[b][:, :], in1=sg[j][0], op=ALU.mult)
                            tt.append((tv, tvt))
                        I("dve", "tensor_tensor", tt[0][1] + tt[1][1], [A_t[cc][n]], out=A[:, cc, ns], in0=tt[0][0], in1=tt[1][0], op=ALU.add)

        def out_proj(seq):
            Wo = [wload(w_out[:, ch * 512:(ch + 1) * 512], 8, 512) for ch in range(2)]
            for tb in range(NB):
                xs, xst = dv(2 * (tb % 2), 4096, F32)
                r0 = seq * S + tb * 128
                I("sp", "dma_start", [], xst, out=xs, in_=x_d[r0:r0 + 128, :], dma=True, semkey="d_xs%d" % (tb % 2))
                for ch in range(2):
                    b = nextbank()
                    W, wt, _ = Wo[ch]
                    mm_group(b, banks[b][:, :], [(A[:, kc, tb * 128:(tb + 1) * 128], W[:, kc, :]) for kc in range(8)],
                             [wt] + [A_t[kc][tb // 4] for kc in range(8)])
                    wr = [x1_t[tb]] + (allB_old if (tb == 0 and ch == 0) else [])
                    I("dve", "tensor_tensor", [bkt[b]] + xst, wr, out=x1[:, tb, ch * 512:(ch + 1) * 512], in0=banks[b][:, :],
                      in1=xs[:, ch * 512:(ch + 1) * 512], op=ALU.add)

        def mlp(ns):
            allA = [t for c in range(8) for t in A_t[c]]
            for g in range(8):
                Wu, wut, _ = wload(w_up[:, g * 512:(g + 1) * 512], 8, 512)
                Wd, wdt, _ = wload(w_down[g * 512:(g + 1) * 512, :], 4, 1024)
                for n in range(4):
                    if g == 0:
                        ns.upto(4 * n + 4)
                    hsel = (g * 4 + n) % 2
                    hd = hid[hsel]
                    hdt = [tok("hid%d_%d" % (hsel, c)) for c in range(4)]
                    for c in range(4):
                        b = nextbank()
                        mm_group(b, banks[b][:, :], [(Wu[:, kc, c * 128:(c + 1) * 128], hT[:, kc, n * 512:(n + 1) * 512]) for kc in range(8)],
                                 [wut] + hT_t[4 * n:4 * n + 4])
                        sq, sqt = dvh(8 + (c % 2), 1024, BF16)
                        I("act", "activation", [bkt[b]], sqt, out=sq, in_=banks[b][:, :], func=AF.Square)
                        wr = [hdt[c]] + (allA if (g == 0 and n == 0 and c == 0) else [])
                        I("dve", "scalar_tensor_tensor", [bkt[b]] + sqt, wr, out=hd[:, c, :], in0=banks[b][:, :], scalar=0.0, in1=sq,
                          op0=ALU.is_gt, op1=ALU.mult)
                    if g == 0 and n < 3:
                        ns.upto(4 * n + 6)
                    for tl in range(4):
                        tb = 4 * n + tl
                        for ch in range(2):
                            b = nextbank2()
                            mm_group(b, banks[b][:, :], [(hd[:, kc, tl * 128:(tl + 1) * 128], Wd[:, kc, ch * 512:(ch + 1) * 512]) for kc in range(4)],
                                     [wdt] + hdt)
                            I("dve", "tensor_tensor", [bkt[b], x1_t[tb]], [x1_t[tb]], out=x1[:, tb, ch * 512:(ch + 1) * 512],
                              in0=banks[b][:, :], in1=x1[:, tb, ch * 512:(ch + 1) * 512], op=ALU.add)

        store_t = []
        gt2 = view(Aoff + 16384, 4096, F32)

        def ple_final(seq, ns):
            allA = [t for c in range(8) for t in A_t[c]]
            pT_t = [tok("pT%d" % tb) for tb in range(NB)]
            I("sp", "dma_start", [], [tok("gt2")] + allA, out=gt2, in_=AP(g_final.tensor, 0, [[0, 128], [1, DM]]), dma=True)
            for tb in range(NB):
                pin, pint = dvh(12 + (tb % 2), 1024, F32)
                r0 = seq * S + tb * 128
                I("sp", "dma_start", [], pint, out=pin, in_=p_d[r0:r0 + 128, :], dma=True, semkey="d_pin%d" % (tb % 2))
                pbf, pbt = dvh(14 + (tb % 2), 512, BF16)
                I("dve", "tensor_copy", pint, pbt, out=pbf, in_=pin)
                for c in range(2):
                    I("pe", "transpose", pbt + [tok("ident")], [bkt[7]], out=tpb[:, c, :], in_=pbf[:, c * 128:(c + 1) * 128], identity=ident)
                I("act", "activation", [bkt[7]], [pT_t[tb]] + (allA if tb == 0 else []), out=pT[:, :, tb * 128:(tb + 1) * 128],
                  in_=tpb[:, 0:2, :], func=AF.Copy)
            Wpg = [wload(w_pg[:, ch * 512:(ch + 1) * 512], 8, 512) for ch in range(2)]
            Wp, wpt, _ = wload(w_ple[:, :], 2, 1024)
            ns.upto(2)
            for tb in range(NB):
                for ch in range(2):
                    b = nextbank()
                    W, wt, _ = Wpg[ch]
                    mm_group(b, banks[b][:, :], [(hT[:, kc, tb * 128:(tb + 1) * 128], W[:, kc, :]) for kc in range(8)], [wt, hT_t[tb]])
                    sgv, sgt = view(Aoff + 20480 + 2048 * ch, 2048, F32), [tok("sgA%d" % ch)]
                    I("act", "activation", [bkt[b]], sgt, out=sgv, in_=banks[b][:, :], func=AF.Sigmoid)
                    b2 = nextbank2()
                    mm_group(b2, banks[b2][:, :], [(pT[:, kc, tb * 128:(tb + 1) * 128], Wp[:, kc, ch * 512:(ch + 1) * 512]) for kc in range(2)],
                             [wpt, pT_t[tb]])
                    I("dve", "tensor_tensor", [bkt[b2]] + sgt, sgt, out=sgv, in0=banks[b2][:, :], in1=sgv, op=ALU.mult)
                    I("dve", "tensor_tensor", sgt + [x1_t[tb]], [x1_t[tb]], out=x1[:, tb, ch * 512:(ch + 1) * 512], in0=sgv,
                      in1=x1[:, tb, ch * 512:(ch + 1) * 512], op=ALU.add)
                ns.upto(tb + 3)
                xin = x1[:, tb, :]
                fs = 16 + tb % 2
                junk, jt = view(Aoff + 24576, 2048, BF16), [tok("fjunk")]
                r0 = seq * S + tb * 128
                sst = tok("fss%d" % (tb % 2))
                I("act", "activation", [x1_t[tb]], jt + [sst], out=junk, in_=xin, func=AF.Square, accum_out=fstat[:, 0 + (tb % 2):1 + (tb % 2)])
                I("act", "activation", [sst], [tok("fln%d" % (tb % 2))], out=fstat[:, 2 + (tb % 2):3 + (tb % 2)],
                  in_=fstat[:, 0 + (tb % 2):1 + (tb % 2)], func=AF.Ln, scale=1.0 / DM, bias=EPS)
                I("act", "activation", [tok("fln%d" % (tb % 2))], [tok("frs%d" % (tb % 2))], out=fstat[:, 4 + (tb % 2):5 + (tb % 2)],
                  in_=fstat[:, 2 + (tb % 2):3 + (tb % 2)], func=AF.Exp, scale=-0.5)
                ob, obt = dv(2 + 2 * (tb % 2), 4096, F32)
                I("dve", "scalar_tensor_tensor", [x1_t[tb], tok("frs%d" % (tb % 2)), tok("gt2")], obt, out=ob, in0=xin,
                  scalar=fstat[:, 4 + (tb % 2):5 + (tb % 2)], in1=gt2, op0=ALU.mult, op1=ALU.mult)
                st = tok("store%d_%d" % (seq, tb))
                store_t.append(st)
                I("sp", "dma_start", obt, [st], out=out_d[r0:r0 + 128, :], in_=ob, dma=True, semkey="d_store%d" % (tb % 2))

        def dump(name, src, toks):
            if DBG and seq == 0:
                dt_ = tok("dbgtok_" + name)
                store_t.append(dt_)
                I("sp", "dma_start", toks, [dt_], out=dbg[name], in_=src, dma=True, semkey="d_dbg_" + name)
        allA_ = [t for c in range(8) for t in A_t[c]]
        alloT = [[t for m in range(4) for t in oT_t[b][m]] for b in range(2)]
        for seq in range(NSEQ):
            in_proj(0, NormStream(seq, True, g_mix))
            dump("hT", hT, hT_t)
            dump("A", A, allA_)
            dump("V", V, V_t)
            forget_tables()
            fox_attention()
            dump("oTf", oT[0], alloT[0])
            in_proj(1)
            sb_attention()
            dump("oTs", oT[1], alloT[1])
            gates_merge()
            out_proj(seq)
            dump("x1a", x1, x1_t)
            mlp(NormStream(seq, False, g_mlp))
            dump("x1b", x1, x1_t)
            ple_final(seq, NormStream(seq, False, g_ple))
        P.add("sp", None, reads=store_t)

        counters = P.assign()
        semh = {k: es.enter_context(nc.semaphore("s_" + k)) for k in counters}
        with nc.Block() as block:
            @block.tensor
            def _(e):
                P.emit_engine("pe", e, semh)

            @block.scalar
            def _(e):
                P.emit_engine("act", e, semh)

            @block.vector
            def _(e):
                P.emit_engine("dve", e, semh)

            @block.gpsimd
            def _(e):
                P.emit_engine("pool", e, semh)

            @block.sync
            def _(e):
                P.emit_engine("sp", e, semh)
    return nc


def kernel(x, p, g_mix, w_in, b_forget, b_gate, w_branch_fox, w_branch_sb, w_out,
           g_mlp, w_up, w_down, g_ple, w_ple_gate, w_ple, g_final):
    f = lambda a: np.ascontiguousarray(np.asarray(a, dtype=np.float32))
    x = f(x)
    p = f(p)
    common = {
        "w_in": f(w_in)[0], "b_forget": f(b_forget).reshape(1, 8), "b_gate": f(b_gate)[0],
        "w_bf": f(w_branch_fox)[0], "w_bs": f(w_branch_sb)[0], "w_out": f(w_out)[0],
        "g_mix": f(g_mix).reshape(1, DM), "g_mlp": f(g_mlp).reshape(1, DM), "g_ple": f(g_ple).reshape(1, DM),
        "g_final": f(g_final).reshape(1, DM), "w_up": f(w_up)[0], "w_down": f(w_down)[0],
        "w_pg": f(w_ple_gate)[0], "w_ple": f(w_ple)[0],
    }
    in_maps = []
    for c in range(NCORES):
        m = dict(common)
        m["x"] = x[c * NSEQ:(c + 1) * NSEQ].reshape(NSEQ * S, DM)
        m["p"] = p[0, c * NSEQ:(c + 1) * NSEQ].reshape(NSEQ * S, 256)
        in_maps.append(m)
    nc = build_program()
    res = run_bass_kernel_spmd(nc, in_maps, core_ids=list(range(NCORES)))
    out = np.stack([np.asarray(r["out"]).reshape(NSEQ, S, DM) for r in res.results], axis=0)
    return out.reshape(NCORES * NSEQ, S, DM).astype(np.float32)
```

```python
from contextlib import ExitStack
import numpy as np
import concourse.bass as bass
import concourse.mybir as mybir
from concourse.bass_utils import run_bass_kernel_spmd
from concourse.ap import AP

F32 = mybir.dt.float32
BF16 = mybir.dt.bfloat16
U8 = mybir.dt.uint8
AF = mybir.ActivationFunctionType
ALU = mybir.AluOpType

NCORES = 8
S = 2048
DM = 1024
NB = 16
NSEQ = 2
EPS = 1e-6
QF, KF, VF, FF, QS, KS, VS, GA, GB = 0, 512, 1024, 1536, 1544, 2056, 2568, 3080, 4104
NEG = -30000.0


class Buf:
    __slots__ = ("name", "last_w", "readers")

    def __init__(self, name):
        self.name = name
        self.last_w = None
        self.readers = []


class Op:
    __slots__ = ("eng", "fn", "deps", "signal", "semkey", "semval", "dma", "idx", "waits", "clock")


class Prog:
    ENGS = ("pe", "act", "dve", "pool", "sp")

    def __init__(self):
        self.ops = {e: [] for e in self.ENGS}
        self.n = 0

    def add(self, eng, fn, reads=(), writes=(), dma=False, semkey=None):
        op = Op()
        op.eng, op.fn, op.dma, op.idx = eng, fn, dma, self.n
        self.n += 1
        op.signal = False
        op.semval = None
        if dma:
            op.semkey = semkey if semkey is not None else "d_" + (writes[0].name if writes else reads[0].name)
        else:
            op.semkey = eng
        deps = {}
        for b in reads:
            d = b.last_w
            if d is not None:
                deps[d.idx] = (d, True)
        for b in writes:
            d = b.last_w
            if d is not None and d.idx not in deps:
                deps[d.idx] = (d, False)
            for r in b.readers:
                if r.idx not in deps:
                    deps[r.idx] = (r, False)
        keep = []
        for d, raw in deps.values():
            if d.dma or dma or d.eng != eng:
                keep.append(d)
            elif eng == "pe":
                continue
            else:
                keep.append(d)
        for d in keep:
            d.signal = True
        op.deps = keep
        for b in reads:
            if not dma:
                b.readers = [r for r in b.readers if r.dma or r.eng != eng]
            b.readers.append(op)
        for b in writes:
            b.last_w = op
            b.readers = []
        self.ops[eng].append(op)
        return op

    def ins(self, eng, meth, reads, writes, *args, dma=False, semkey=None, **kwargs):
        def fn(e, meth=meth, args=args, kwargs=kwargs):
            return getattr(e, meth)(*args, **kwargs)
        return self.add(eng, fn, reads=reads, writes=writes, dma=dma, semkey=semkey)

    def assign(self):
        counters = {}
        allops = sorted((op for e in self.ENGS for op in self.ops[e]), key=lambda o: o.idx)
        for op in allops:
            if op.signal:
                inc = 16 if op.dma else 1
                counters[op.semkey] = counters.get(op.semkey, 0) + inc
                op.semval = counters[op.semkey]
        known = {e: {} for e in self.ENGS}
        for op in allops:
            K = known[op.eng]
            waits = []
            for d in sorted(op.deps, key=lambda o: -o.idx):
                if K.get(d.semkey, 0) < d.semval:
                    waits.append((d.semkey, d.semval))
                    K[d.semkey] = d.semval
                for k, v in d.clock.items():
                    if K.get(k, 0) < v:
                        K[k] = v
            best = {}
            for k, v in waits:
                if best.get(k, 0) < v:
                    best[k] = v
            op.waits = list(best.items())
            if op.signal:
                op.clock = dict(K)
                op.clock[op.semkey] = op.semval
            else:
                op.clock = None
        return counters

    def emit_engine(self, eng_name, eng, semh):
        for op in self.ops[eng_name]:
            todo = list(op.waits)
            attach = todo.pop() if (todo and op.fn is not None) else None
            for k, v in todo:
                eng.wait_ge(semh[k], v)
            ins = op.fn(eng) if op.fn is not None else None
            if attach is not None:
                ins._wait_ge(semh[attach[0]], attach[1])
            if op.signal:
                if ins is None:
                    ins = eng.nop()
                ins.then_inc(semh[op.semkey], 16 if op.dma else 1)


def build_program():
    nc = bass.Bass("TRN2", target_bir_lowering=False)
    dt_in = lambda n, shp: nc.dram_tensor(n, shp, F32, kind="ExternalInput").ap()
    x_d = dt_in("x", [NSEQ * S, DM])
    p_d = dt_in("p", [NSEQ * S, 256])
    w_in = dt_in("w_in", [DM, 5128])
    b_forget = dt_in("b_forget", [1, 8])
    b_gate = dt_in("b_gate", [2, DM])
    w_bf = dt_in("w_bf", [512, DM])
    w_bs = dt_in("w_bs", [512, DM])
    w_out = dt_in("w_out", [DM, DM])
    g_mix = dt_in("g_mix", [1, DM])
    g_mlp = dt_in("g_mlp", [1, DM])
    g_ple = dt_in("g_ple", [1, DM])
    g_final = dt_in("g_final", [1, DM])
    w_up = dt_in("w_up", [DM, 4096])
    w_down = dt_in("w_down", [4096, DM])
    w_pg = dt_in("w_pg", [DM, DM])
    w_ple = dt_in("w_ple", [256, DM])
    out_d = nc.dram_tensor("out", [NSEQ * S, DM], F32, kind="ExternalOutput").ap()

    P = Prog()
    es = ExitStack()
    with es:
        ARENA = 205 * 1024
        arena = es.enter_context(nc.sbuf_tensor("arena", [128, ARENA], U8))
        cur = [0]

        def carve(nbytes):
            off = cur[0]
            cur[0] += (nbytes + 63) // 64 * 64
            assert cur[0] <= ARENA, cur[0]
            return off

        def view(off, nbytes, dt, pat=None, **kw):
            v = arena[:, off:off + nbytes].bitcast(dt)
            if pat is not None:
                v = v.rearrange(pat, **kw)
            return v

        def tile(nbytes, dt, pat=None, **kw):
            return view(carve(nbytes), nbytes, dt, pat, **kw)

        ident = tile(256, BF16)
        identf = tile(512, F32)
        ntri = tile(256, BF16)
        tmo = tile(256, BF16)
        nmF = tile(256, BF16)
        nmS = tile(256, BF16)
        zl = tile(256, BF16)
        zr = tile(1024, BF16)
        onesb = tile(256, BF16)
        negb = tile(256, BF16)
        ntriincf = tile(512, F32)
        negonesf = tile(512, F32)
        negf = tile(512, F32)
        bfb = tile(512, F32)
        bgT = tile(64, F32, "p (j c) -> p j c", j=2)
        gt = tile(4096, F32)
        wf = tile(128, BF16, "p (k c) -> p k c", k=8)
        ss = tile(64, F32)
        fstat = tile(64, F32)
        lnv = tile(64, F32)
        rstd = tile(64, F32)
        cfull = tile(512, F32, "p (t h) -> p t h", t=16)
        cpre = tile(512, F32, "p (t h) -> p t h", t=16)
        totsb = tile(512, F32, "p (t h) -> p t h", t=16)
        tF = tile(512, F32)
        eF = tile(512, F32)
        spF = tile(512, F32)
        B = {}

        def tok(n):
            if n not in B:
                B[n] = Buf(n)
            return B[n]

        NSLOT = 5
        slots = [tile(8192, BF16) for _ in range(NSLOT)]
        hT = tile(32768, BF16, "p (c t) -> p c t", c=8)
        Aoff = carve(32768)
        A = view(Aoff, 32768, BF16, "p (c t) -> p c t", c=8)
        pT = view(Aoff, 8192, BF16, "p (c t) -> p c t", c=2)
        hid = [view(Aoff + 8192 + i * 4096, 4096, BF16, "p (c t) -> p c t", c=4) for i in range(2)]
        Boff = carve(65536)
        V = view(Boff, 24576, BF16, "p (t m c) -> p t m c", t=16, m=4)
        oT = [view(Boff + 24576 + i * 16384, 16384, BF16, "p (c t) -> p c t", c=4) for i in range(2)]
        x1 = view(Boff, 65536, F32, "p (t c) -> p t c", t=16)
        ND = 10
        Doff = carve(ND * 2048)
        Dh = [tok("Dh%d" % i) for i in range(2 * ND)]

        def dv(chunk, nbytes, dt, pat=None, **kw):
            nch = (nbytes + 2047) // 2048
            assert chunk + nch <= ND
            return view(Doff + chunk * 2048, nbytes, dt, pat, **kw), Dh[2 * chunk:2 * (chunk + nch)]

        def dvh(half, nbytes, dt, pat=None, **kw):
            nh = (nbytes + 1023) // 1024
            assert half + nh <= 2 * ND
            return view(Doff + half * 1024, nbytes, dt, pat, **kw), Dh[half:half + nh]

        banks = [es.enter_context(nc.psum_tensor("bank%d" % i, [128, 512], F32)) for i in range(8)]
        bkt = [tok("bank%d" % i) for i in range(8)]
        tpb = banks[7][:, :].bitcast(BF16).rearrange("p (c t) -> p c t", c=8)
        rot = {"mm": 0}

        hT_t = [tok("hT%d" % i) for i in range(NB)]
        A_t = [[tok("A%d_%d" % (c, n)) for n in range(4)] for c in range(8)]
        V_t = [tok("V%d" % i) for i in range(NB)]
        oT_t = [[[tok("oT%d_%d_%d" % (b, m, n)) for n in range(4)] for m in range(4)] for b in range(2)]
        x1_t = [tok("x1_%d" % i) for i in range(NB)]
        slot_t = [tok("slot%d" % i) for i in range(NSLOT)]
        allB_old = V_t + [t for b in range(2) for m in range(4) for t in oT_t[b][m]]
        slot_i = [0]

        I = P.ins

        def wload(src2d, kc, cols, into=None, kofs=0):
            if into is None:
                si = slot_i[0] % NSLOT
                slot_i[0] += 1
            else:
                si = into
            kct = 4096 // cols
            vw = slots[si].rearrange("p (k c) -> p k c", k=kct)
            I("pool", "dma_start", [], [slot_t[si]], out=vw[:, kofs:kofs + kc, :],
              in_=src2d.rearrange("(k p) c -> p k c", p=128), dma=True)
            return vw, slot_t[si], si

        def mm_group(bank_i, out_ap, pairs, reads):
            n = len(pairs)
            for i, (l, r) in enumerate(pairs):
                I("pe", "matmul", reads, [bkt[bank_i]], out_ap, lhsT=l, rhs=r, start=(i == 0), stop=(i == n - 1))

        def nextbank(nrot=4):
            b = rot["mm"] % nrot
            rot["mm"] += 1
            return b

        def nextbank2():
            b = 4 + rot["mm2"] % 4
            rot["mm2"] += 1
            return b
        rot["mm2"] = 0

        I("dve", "memset", [], [tok("onesb")], onesb, 1.0)
        I("dve", "memset", [], [tok("negb")], negb, -1.0)
        I("dve", "memset", [], [tok("zl")], zl, 0.0)
        I("dve", "memset", [], [tok("zr")], zr, 0.0)
        I("dve", "memset", [], [tok("negf")], negf, -1.0)
        I("dve", "memset", [], [tok("negonesf")], negonesf, -1.0)

        def asel(out, in_, pat, cm, op, fill, rd, wr):
            I("pool", "affine_select", [tok(rd)], [tok(wr)], out=out, in_=in_, pattern=pat, compare_op=op, fill=fill, base=0,
              channel_multiplier=cm)
        asel(ident, onesb, [[-1, 128]], 1, ALU.is_equal, 0.0, "onesb", "ident")
        asel(ntri, negb, [[-1, 128]], 1, ALU.is_ge, 0.0, "negb", "ntri")
        asel(tmo, negb, [[1, 128]], -1, ALU.is_gt, 0.0, "negb", "tmo")
        asel(nmF, zl, [[1, 128]], -1, ALU.is_ge, NEG, "zl", "nmF")
        asel(nmS, zl, [[1, 128]], -1, ALU.is_gt, NEG, "zl", "nmS")
        asel(ntriincf, negf, [[1, 128]], -1, ALU.is_ge, 0.0, "negf", "ntriincf")
        I("dve", "tensor_copy", [tok("ident")], [tok("identf")], out=identf, in_=ident)
        bf8 = tile(64, F32)
        I("sp", "dma_start", [], [tok("bf8")], out=bf8[:, 0:8], in_=AP(b_forget.tensor, 0, [[0, 128], [1, 8]]), dma=True)
        bf8b = AP(bf8.tensor, bf8.offset, [list(bf8.ap[0]), [0, 16], [1, 8]])
        I("dve", "tensor_copy", [tok("bf8")], [tok("bfb")], out=bfb.rearrange("p (t h) -> p t h", t=16), in_=bf8b)
        I("sp", "dma_start", [], [tok("bgT")], out=bgT, in_=b_gate.rearrange("j (c p) -> p j c", p=128),
          allow_slow_non_contiguous=True, dma=True)

        def load_g(g_ap):
            I("sp", "dma_start", [], [tok("gt")], out=gt, in_=AP(g_ap.tensor, 0, [[0, 128], [1, DM]]), dma=True)

        def rms_stats(xin, xr, tb):
            junk, jt = dv(8, 2048, BF16)
            sst = tok("ss%d" % tb)
            I("act", "activation", xr, jt + [sst], out=junk, in_=xin, func=AF.Square, accum_out=ss[:, tb:tb + 1])
            I("act", "activation", [sst], [tok("lnv%d" % tb)], out=lnv[:, tb:tb + 1], in_=ss[:, tb:tb + 1], func=AF.Ln,
              scale=1.0 / DM, bias=EPS)
            I("act", "activation", [tok("lnv%d" % tb)], [tok("rstd%d" % tb)], out=rstd[:, tb:tb + 1], in_=lnv[:, tb:tb + 1],
              func=AF.Exp, scale=-0.5)

        def norm_A(seq, tb, from_dram):
            if from_dram:
                xin, xr = dv(4 + 2 * (tb % 2), 4096, F32)
                r0 = seq * S + tb * 128
                I("sp", "dma_start", [], xr, out=xin, in_=x_d[r0:r0 + 128, :], dma=True, semkey="d_xin%d" % (tb % 2))
            else:
                xin, xr = x1[:, tb, :], [x1_t[tb]]
            rms_stats(xin, xr, tb)
            hbf, ht = dv(tb % 2, 2048, BF16)
            I("dve", "scalar_tensor_tensor", xr + [tok("rstd%d" % tb), tok("gt")], ht, out=hbf, in0=xin, scalar=rstd[:, tb:tb + 1],
              in1=gt, op0=ALU.mult, op1=ALU.mult)

        def norm_B(tb):
            hbf, ht = dv(tb % 2, 2048, BF16)
            for c in range(8):
                I("pe", "transpose", ht + [tok("ident")], [bkt[7]], out=tpb[:, c, :], in_=hbf[:, c * 128:(c + 1) * 128], identity=ident)
            if tb % 2 == 0:
                I("act", "activation", [bkt[7]], [hT_t[tb]], out=hT[:, :, tb * 128:(tb + 1) * 128], in_=tpb, func=AF.Copy)
            else:
                I("dve", "tensor_copy", [bkt[7]], [hT_t[tb]], out=hT[:, :, tb * 128:(tb + 1) * 128], in_=tpb)

        class NormStream:
            def __init__(self, seq, from_dram, g_ap):
                load_g(g_ap)
                self.seq, self.fd, self.a, self.b = seq, from_dram, 0, 0

            def upto(self, nb):
                nb = min(nb, NB)
                while self.b < nb:
                    while self.a < min(self.b + 2, NB):
                        norm_A(self.seq, self.a, self.fd)
                        self.a += 1
                    norm_B(self.b)
                    self.b += 1
                    if self.a < NB and self.a < self.b + 2:
                        norm_A(self.seq, self.a, self.fd)
                        self.a += 1

        def in_proj(br, ns=None):
            qo, ko, vo = (QF, KF, VF) if br == 0 else (QS, KS, VS)
            Wq, wqt, _ = wload(w_in[:, qo:qo + 512], 8, 512)
            Wk, wkt, _ = wload(w_in[:, ko:ko + 512], 8, 512)
            Wv, wvt, _ = wload(w_in[:, vo:vo + 512], 8, 512)
            if br == 0:
                I("pool", "dma_start", [], [tok("wf")], out=wf, in_=w_in[:, FF:FF + 8].rearrange("(k p) c -> p k c", p=128), dma=True)
            I("dve", "memset", [], V_t + (x1_t if br == 0 else []), V[:, :, :, 64:128], 1.0 if br == 0 else 0.0)
            for n in range(4):
                if ns is not None:
                    ns.upto(4 * n + 4)
                for which, W, wt in ((0, Wq, wqt), (1, Wk, wkt)):
                    for m in range(4):
                        b = nextbank(4)
                        mm_group(b, banks[b][:, :], [(W[:, kc, m * 128:(m + 1) * 128], hT[:, kc, n * 512:(n + 1) * 512]) for kc in range(8)],
                                 [wt] + hT_t[4 * n:4 * n + 4])
                        dst = A[:, which * 4 + m, n * 512:(n + 1) * 512]
                        if which == 0:
                            I("act", "activation", [bkt[b]], [A_t[m][n]], out=dst, in_=banks[b][:, :], func=AF.Copy, scale=0.125)
                        else:
                            I("dve", "tensor_copy", [bkt[b]], [A_t[4 + m][n]], out=dst, in_=banks[b][:, :])
                for tb in range(4 * n, 4 * n + 4):
                    b = nextbank(4)
                    mm_group(b, banks[b][:, :], [(hT[:, kc, tb * 128:(tb + 1) * 128], Wv[:, kc, :]) for kc in range(8)], [wvt, hT_t[tb]])
                    dst = V[:, tb, :, :].rearrange("p m (a c) -> p m a c", a=3)[:, :, 0:3:2, :]
                    src = banks[b][:, :].rearrange("p (m a c) -> p m a c", m=4, a=2)
                    if tb % 2 == 0:
                        I("act", "activation", [bkt[b]], [V_t[tb]], out=dst, in_=src, func=AF.Copy)
                    else:
                        I("dve", "tensor_copy", [bkt[b]], [V_t[tb]], out=dst, in_=src)

        def forget_tables():
            fb = 7
            for tb in range(NB):
                mm_group(fb, banks[fb][:, tb * 8:(tb + 1) * 8], [(hT[:, kc, tb * 128:(tb + 1) * 128], wf[:, kc, :]) for kc in range(8)],
                         [tok("wf"), hT_t[tb]])
            I("dve", "tensor_tensor", [bkt[fb], tok("bfb")], [tok("tF")], out=tF, in0=banks[fb][:, 0:128], in1=bfb, op=ALU.add)
            I("act", "activation", [tok("tF")], [tok("eF")], out=eF, in_=tF, func=AF.Exp, scale=-1.0)
            I("act", "activation", [tok("eF")], [tok("spF")], out=spF, in_=eF, func=AF.Ln, bias=1.0)
            I("pe", "matmul", [tok("spF"), tok("ntriincf")], [bkt[fb]], banks[fb][:, 128:256], lhsT=ntriincf, rhs=spF, start=True, stop=True)
            I("pe", "matmul", [tok("spF"), tok("negonesf")], [bkt[fb]], banks[fb][:, 256:384], lhsT=negonesf, rhs=spF, start=True, stop=True)
            I("dve", "tensor_copy", [bkt[fb]], [tok("totsb")], out=totsb, in_=banks[fb][:, 256:384].rearrange("p (t h) -> p t h", t=16))
            I("dve", "memset", [], [tok("cpre")], cpre[:, 0, :], 0.0)
            for tb in range(1, NB):
                I("dve", "tensor_tensor", [tok("cpre"), tok("totsb")], [tok("cpre")], out=cpre[:, tb, :], in0=cpre[:, tb - 1, :],
                  in1=totsb[:, tb - 1, :], op=ALU.add)
            I("dve", "tensor_tensor", [bkt[fb], tok("cpre")], [tok("cfull")], out=cfull,
              in0=banks[fb][:, 128:256].rearrange("p (t h) -> p t h", t=16), in1=cpre, op=ALU.add)

        def run_pipe(tiles, stages):
            maxd = max(d for d, _ in stages)
            for t in range(len(tiles) + maxd):
                for d, fn in stages:
                    j = t - d
                    if 0 <= j < len(tiles):
                        fn(tiles[j])

        def fox_block(m, n, par):
            XY = (4 + 2 * par, 5 + 2 * par)
            cts = []
            for hl in range(2):
                h = 2 * m + hl
                ct, ctt = dv(hl + 2 * par, 2048, F32)
                cb = nextbank(4)
                for j in range(4):
                    qb = 4 * n + j
                    I("pe", "matmul", [tok("cfull"), tok("identf")], [bkt[cb]], banks[cb][:, j * 128:(j + 1) * 128],
                      lhsT=cfull[:, qb, h:h + 1].to_broadcast([128, 128]), rhs=identf, start=True, stop=True)
                I("act", "activation", [bkt[cb]], ctt, out=ct, in_=banks[cb][:, :], func=AF.Copy)
                cts.append((ct, ctt))
            nkb = 4 * (n + 1)
            tiles = []
            for kb in range(nkb):
                i = kb - 4 * n
                tiles.append(dict(kb=kb, i=i, c0=max(i, 0) * 128, idx=len(tiles)))

            def st_z(T):
                kb, c0, i = T["kb"], T["c0"], T["i"]
                T["zb"] = [nextbank(4), nextbank(4)]
                for hl in range(2):
                    b = T["zb"][hl]
                    pl = slice(hl * 64, hl * 64 + 64)
                    I("pe", "matmul", [A_t[4 + m][kb // 4], A_t[m][n]], [bkt[b]], banks[b][:, c0:512],
                      lhsT=A[pl, 4 + m, kb * 128:(kb + 1) * 128], rhs=A[pl, m, n * 512 + c0:(n + 1) * 512], start=True, stop=(i < 0))
                if i >= 0:
                    for hl in range(2):
                        b = T["zb"][hl]
                        I("pe", "matmul", [tok("ident"), tok("nmF")], [bkt[b]], banks[b][:, c0:c0 + 128], lhsT=ident, rhs=nmF,
                          start=False, stop=True)

            def st_lg(T):
                kb, c0 = T["kb"], T["c0"]
                T["lg"] = []
                for hl in range(2):
                    b = T["zb"][hl]
                    h = 2 * m + hl
                    lg, lgt = dv(4 + (2 * T["idx"] + hl) % 3, 2048, F32)
                    T["lg"].append((lg, lgt))
                    ct, ctt = cts[hl]
                    I("dve", "scalar_tensor_tensor", [bkt[b], tok("cfull")] + ctt, lgt, out=lg[:, c0:512], in0=banks[b][:, c0:512],
                      scalar=cfull[:, kb, h:h + 1], in1=ct[:, c0:512], op0=ALU.subtract, op1=ALU.add)

            def st_exp(T):
                c0 = T["c0"]
                T["pt"] = []
                for hl in range(2):
                    pt, ptt = dvh(14 + (2 * T["idx"] + hl) % 6, 1024, BF16)
                    T["pt"].append((pt, ptt))
                    lg, lgt = T["lg"][hl]
                    I("act", "activation", lgt, ptt, out=pt[:, c0:512], in_=lg[:, c0:512], func=AF.Exp)

            def st_pv(T):
                kb, c0 = T["kb"], T["c0"]
                for hl in range(2):
                    b = XY[hl]
                    pt, ptt = T["pt"][hl]
                    I("pe", "matmul", [V_t[kb]] + ptt, [bkt[b]], banks[b][:, c0:512], lhsT=V[:, kb, m, hl * 64:hl * 64 + 128],
                      rhs=pt[:, c0:512], start=(kb == 0), stop=(kb == nkb - 1))

            run_pipe(tiles, [(0, st_z), (0, st_lg), (0, st_exp), (2, st_pv)])
            X, Y = banks[XY[0]], banks[XY[1]]
            den, dent = dv(4, 2048, F32)
            rec, rect = dv(5, 2048, F32)
            I("dve", "tensor_copy", [bkt[XY[0]]], dent, out=den[0:64, :], in_=X[64:128, :])
            I("dve", "tensor_copy", [bkt[XY[1]]], dent, out=den[64:128, :], in_=Y[0:64, :])
            I("act", "activation", dent, rect, out=rec, in_=den, func=AF.Ln)
            I("act", "activation", rect, dent, out=den, in_=rec, func=AF.Exp, scale=-1.0)
            dst = oT[0][:, m, n * 512:(n + 1) * 512]
            I("dve", "tensor_tensor", [bkt[XY[0]]] + dent, [oT_t[0][m][n]], out=dst[0:64, :], in0=X[0:64, :], in1=den[0:64, :], op=ALU.mult)
            I("dve", "tensor_tensor", [bkt[XY[1]]] + dent, [oT_t[0][m][n]], out=dst[64:128, :], in0=Y[64:128, :], in1=den[64:128, :],
              op=ALU.mult)

        def fox_attention():
            k = 0
            for m in range(4):
                for n in range(4):
                    fox_block(m, n, k % 2)
                    k += 1

        def sb_block(m, n, par):
            Bk = (4, 5)
            Ob = 6 + par
            for b in (Bk[0], Bk[1], Ob):
                I("pe", "matmul", [tok("zl"), tok("zr")], [bkt[b]], banks[b][:, :], lhsT=zl, rhs=zr, start=True, stop=(b != Ob),
                  skip_group_check=True)
            nkb = 4 * (n + 1)
            tiles = []
            for kb in reversed(range(nkb)):
                i = kb - 4 * n
                tiles.append(dict(kb=kb, i=i, c0=max(i, 0) * 128, idx=len(tiles)))
            ntl = len(tiles)

            def st_z(T):
                kb, c0, i = T["kb"], T["c0"], T["i"]
                T["zb"] = [nextbank(4), nextbank(4)]
                for hl in range(2):
                    b = T["zb"][hl]
                    pl = slice(hl * 64, hl * 64 + 64)
                    I("pe", "matmul", [A_t[4 + m][kb // 4], A_t[m][n]], [bkt[b]], banks[b][:, c0:512],
                      lhsT=A[pl, 4 + m, kb * 128:(kb + 1) * 128], rhs=A[pl, m, n * 512 + c0:(n + 1) * 512], start=True, stop=(i < 0))
                if i >= 0:
                    for hl in range(2):
                        b = T["zb"][hl]
                        I("pe", "matmul", [tok("ident"), tok("nmS")], [bkt[b]], banks[b][:, c0:c0 + 128], lhsT=ident, rhs=nmS,
                          start=False, stop=True)

            def st_e(T):
                c0 = T["c0"]
                T["e"], T["L"] = [], []
                for hl in range(2):
                    b = T["zb"][hl]
                    e_, et = dvh((2 * T["idx"] + hl) % 4, 1024, BF16)
                    T["e"].append((e_, et))
                    I("act", "activation", [bkt[b]], et, out=e_[:, c0:512], in_=banks[b][:, c0:512], func=AF.Exp)
                for hl in range(2):
                    e_, et = T["e"][hl]
                    L_, Lt = dvh(4 + (2 * T["idx"] + hl) % 6, 1024, BF16)
                    T["L"].append((L_, Lt))
                    I("act", "activation", et, Lt, out=L_[:, c0:512], in_=e_[:, c0:512], func=AF.Ln, bias=1.0)

            def st_ntri(T):
                c0 = T["c0"]
                for hl in range(2):
                    L_, Lt = T["L"][hl]
                    I("pe", "matmul", [tok("ntri")] + Lt, [bkt[Bk[hl]]], banks[Bk[hl]][:, c0:512], lhsT=ntri, rhs=L_[:, c0:512],
                      start=False, stop=True, skip_group_check=True)

            def st_x(T):
                c0 = T["c0"]
                T["X"] = []
                for hl in range(2):
                    X_, Xt = dvh(10 + (2 * T["idx"] + hl) % 4, 1024, BF16)
                    T["X"].append((X_, Xt))
                    I("act", "activation", [bkt[Bk[hl]]], Xt, out=X_[:, c0:512], in_=banks[Bk[hl]][:, c0:512], func=AF.Exp)

            def st_tmo(T):
                c0 = T["c0"]
                for hl in range(2):
                    L_, Lt = T["L"][hl]
                    I("pe", "matmul", [tok("tmo")] + Lt, [bkt[Bk[hl]]], banks[Bk[hl]][:, c0:512], lhsT=tmo, rhs=L_[:, c0:512],
                      start=False, stop=True, skip_group_check=True)

            def st_at(T):
                c0 = T["c0"]
                T["AT"] = []
                for hl in range(2):
                    AT, ATt = dvh(14 + (2 * T["idx"] + hl) % 4, 1024, BF16)
                    T["AT"].append((AT, ATt))
                    e_, et = T["e"][hl]
                    X_, Xt = T["X"][hl]
                    I("dve", "tensor_tensor", et + Xt, ATt, out=AT[:, c0:512], in0=e_[:, c0:512], in1=X_[:, c0:512], op=ALU.mult)

            def st_pv(T):
                kb, c0 = T["kb"], T["c0"]
                for hl in range(2):
                    AT, ATt = T["AT"][hl]
                    I("pe", "matmul", [V_t[kb]] + ATt, [bkt[Ob]], banks[Ob][:, c0:512], lhsT=V[:, kb, m, hl * 64:hl * 64 + 128],
                      rhs=AT[:, c0:512], start=False, stop=(T["idx"] == ntl - 1 and hl == 1))

            run_pipe(tiles, [(0, st_z), (0, st_e), (2, st_tmo), (1, st_ntri), (1, st_x), (1, st_at), (2, st_pv)])
            I("dve", "tensor_copy", [bkt[Ob]], [oT_t[1][m][n]], out=oT[1][:, m, n * 512:(n + 1) * 512], in_=banks[Ob][:, :])

        def sb_attention():
            k = 0
            for m in range(4):
                for n in range(4):
                    sb_block(m, n, k % 2)
                    k += 1

        def gates_merge():
            for cg in range(2):
                Wga, gat, _ = wload(w_in[:, GA + cg * 512:GA + (cg + 1) * 512], 8, 512)
                Wgb, gbt, _ = wload(w_in[:, GB + cg * 512:GB + (cg + 1) * 512], 8, 512)
                Wbr, brt, si = wload(w_bf[:, cg * 512:(cg + 1) * 512], 4, 512)
                wload(w_bs[:, cg * 512:(cg + 1) * 512], 4, 512, into=si, kofs=4)
                for c in range(4):
                    cc = cg * 4 + c
                    cs = slice(c * 128, (c + 1) * 128)
                    for n in range(4):
                        ns = slice(n * 512, (n + 1) * 512)
                        it = c * 4 + n
                        sg = []
                        for j, (W, wt) in enumerate(((Wga, gat), (Wgb, gbt))):
                            b = nextbank(4)
                            mm_group(b, banks[b][:, :], [(W[:, kc, cs], hT[:, kc, ns]) for kc in range(8)], [wt] + hT_t[4 * n:4 * n + 4])
                            sgv, sgt = dv(2 * (it % 2) + j, 2048, F32)
                            I("act", "activation", [bkt[b], tok("bgT")], sgt, out=sgv, in_=banks[b][:, :], func=AF.Sigmoid,
                              bias=bgT[:, j, cc:cc + 1])
                            sg.append((sgv, sgt))
                        tt = []
                        for j in range(2):
                            b = 4 + 2 * (it % 2) + j
                            mm_group(b, banks[b][:, :], [(Wbr[:, 4 * j + kc, cs], oT[j][:, kc, ns]) for kc in range(4)],
                                     [brt] + [oT_t[j][kc][n] for kc in range(4)])
                            tv, tvt = dv(4 + 2 * (it % 2) + j, 2048, F32)
                            I("dve", "tensor_tensor", [bkt[b]] + sg[j][1], tvt, out=tv, in0=banks[b][:, :], in1=sg[j][0], op=ALU.mult)
                            tt.append((tv, tvt))
                        I("dve", "tensor_tensor", tt[0][1] + tt[1][1], [A_t[cc][n]], out=A[:, cc, ns], in0=tt[0][0], in1=tt[1][0], op=ALU.add)

        def out_proj(seq):
            Wo = [wload(w_out[:, ch * 512:(ch + 1) * 512], 8, 512) for ch in range(2)]
            for tb in range(NB):
                xs, xst = dv(2 * (tb % 2), 4096, F32)
                r0 = seq * S + tb * 128
                I("sp", "dma_start", [], xst, out=xs, in_=x_d[r0:r0 + 128, :], dma=True, semkey="d_xs%d" % (tb % 2))
                for ch in range(2):
                    b = nextbank()
                    W, wt, _ = Wo[ch]
                    mm_group(b, banks[b][:, :], [(A[:, kc, tb * 128:(tb + 1) * 128], W[:, kc, :]) for kc in range(8)],
                             [wt] + [A_t[kc][tb // 4] for kc in range(8)])
                    wr = [x1_t[tb]] + (allB_old if (tb == 0 and ch == 0) else [])
                    I("dve", "tensor_tensor", [bkt[b]] + xst, wr, out=x1[:, tb, ch * 512:(ch + 1) * 512], in0=banks[b][:, :],
                      in1=xs[:, ch * 512:(ch + 1) * 512], op=ALU.add)

        def mlp(ns):
            allA = [t for c in range(8) for t in A_t[c]]
            for g in range(8):
                Wu, wut, _ = wload(w_up[:, g * 512:(g + 1) * 512], 8, 512)
                Wd, wdt, _ = wload(w_down[g * 512:(g + 1) * 512, :], 4, 1024)
                for n in range(4):
                    if g == 0:
                        ns.upto(4 * n + 4)
                    hsel = (g * 4 + n) % 2
                    hd = hid[hsel]
                    hdt = [tok("hid%d_%d" % (hsel, c)) for c in range(4)]
                    for c in range(4):
                        b = nextbank()
                        mm_group(b, banks[b][:, :], [(Wu[:, kc, c * 128:(c + 1) * 128], hT[:, kc, n * 512:(n + 1) * 512]) for kc in range(8)],
                                 [wut] + hT_t[4 * n:4 * n + 4])
                        sq, sqt = dvh(8 + (c % 2), 1024, BF16)
                        I("act", "activation", [bkt[b]], sqt, out=sq, in_=banks[b][:, :], func=AF.Square)
                        wr = [hdt[c]] + (allA if (g == 0 and n == 0 and c == 0) else [])
                        I("dve", "scalar_tensor_tensor", [bkt[b]] + sqt, wr, out=hd[:, c, :], in0=banks[b][:, :], scalar=0.0, in1=sq,
                          op0=ALU.is_gt, op1=ALU.mult)
                    if g == 0 and n < 3:
                        ns.upto(4 * n + 6)
                    for tl in range(4):
                        tb = 4 * n + tl
                        for ch in range(2):
                            b = nextbank2()
                            mm_group(b, banks[b][:, :], [(hd[:, kc, tl * 128:(tl + 1) * 128], Wd[:, kc, ch * 512:(ch + 1) * 512]) for kc in range(4)],
                                     [wdt] + hdt)
                            I("dve", "tensor_tensor", [bkt[b], x1_t[tb]], [x1_t[tb]], out=x1[:, tb, ch * 512:(ch + 1) * 512],
                              in0=banks[b][:, :], in1=x1[:, tb, ch * 512:(ch + 1) * 512], op=ALU.add)

        store_t = []
        gt2 = view(Aoff + 16384, 4096, F32)

        def ple_prep(seq):
            allA = [t for c in range(8) for t in A_t[c]]
            pT_t = [tok("pT%d" % tb) for tb in range(NB)]
            for tb in range(NB):
                pin, pint = dvh(12 + (tb % 2), 1024, F32)
                r0 = seq * S + tb * 128
                I("sp", "dma_start", [], pint, out=pin, in_=p_d[r0:r0 + 128, :], dma=True, semkey="d_pin%d" % (tb % 2))
                pbf, pbt = dvh(14 + (tb % 2), 512, BF16)
                I("dve", "tensor_copy", pint, pbt, out=pbf, in_=pin)
                for c in range(2):
                    I("pe", "transpose", pbt + [tok("ident")], [bkt[7]], out=tpb[:, c, :], in_=pbf[:, c * 128:(c + 1) * 128], identity=ident)
                I("act", "activation", [bkt[7]], [pT_t[tb]] + (allA if tb == 0 else []), out=pT[:, :, tb * 128:(tb + 1) * 128],
                  in_=tpb[:, 0:2, :], func=AF.Copy)

        def ple_final(seq, ns):
            allA = [t for c in range(8) for t in A_t[c]]
            pT_t = [tok("pT%d" % tb) for tb in range(NB)]
            I("sp", "dma_start", [], [tok("gt2")] + allA, out=gt2, in_=AP(g_final.tensor, 0, [[0, 128], [1, DM]]), dma=True)
            Wpg = [wload(w_pg[:, ch * 512:(ch + 1) * 512], 8, 512) for ch in range(2)]
            Wp, wpt, _ = wload(w_ple[:, :], 2, 1024)
            ns.upto(2)
            for tb in range(NB):
                for ch in range(2):
                    b = nextbank()
                    W, wt, _ = Wpg[ch]
                    mm_group(b, banks[b][:, :], [(hT[:, kc, tb * 128:(tb + 1) * 128], W[:, kc, :]) for kc in range(8)], [wt, hT_t[tb]])
                    sgv, sgt = view(Aoff + 20480 + 2048 * ch, 2048, F32), [tok("sgA%d" % ch)]
                    I("act", "activation", [bkt[b]], sgt, out=sgv, in_=banks[b][:, :], func=AF.Sigmoid)
                    b2 = nextbank2()
                    mm_group(b2, banks[b2][:, :], [(pT[:, kc, tb * 128:(tb + 1) * 128], Wp[:, kc, ch * 512:(ch + 1) * 512]) for kc in range(2)],
                             [wpt, pT_t[tb]])
                    I("dve", "tensor_tensor", [bkt[b2]] + sgt, sgt, out=sgv, in0=banks[b2][:, :], in1=sgv, op=ALU.mult)
                    I("dve", "tensor_tensor", sgt + [x1_t[tb]], [x1_t[tb]], out=x1[:, tb, ch * 512:(ch + 1) * 512], in0=sgv,
                      in1=x1[:, tb, ch * 512:(ch + 1) * 512], op=ALU.add)
                ns.upto(tb + 3)
                xin = x1[:, tb, :]
                fs = 16 + tb % 2
                junk, jt = view(Aoff + 24576, 2048, BF16), [tok("fjunk")]
                r0 = seq * S + tb * 128
                sst = tok("fss%d" % (tb % 2))
                I("act", "activation", [x1_t[tb]], jt + [sst], out=junk, in_=xin, func=AF.Square, accum_out=fstat[:, 0 + (tb % 2):1 + (tb % 2)])
                I("act", "activation", [sst], [tok("fln%d" % (tb % 2))], out=fstat[:, 2 + (tb % 2):3 + (tb % 2)],
                  in_=fstat[:, 0 + (tb % 2):1 + (tb % 2)], func=AF.Ln, scale=1.0 / DM, bias=EPS)
                I("act", "activation", [tok("fln%d" % (tb % 2))], [tok("frs%d" % (tb % 2))], out=fstat[:, 4 + (tb % 2):5 + (tb % 2)],
                  in_=fstat[:, 2 + (tb % 2):3 + (tb % 2)], func=AF.Exp, scale=-0.5)
                ob, obt = dv(2 + 2 * (tb % 2), 4096, F32)
                I("dve", "scalar_tensor_tensor", [x1_t[tb], tok("frs%d" % (tb % 2)), tok("gt2")], obt, out=ob, in0=xin,
                  scalar=fstat[:, 4 + (tb % 2):5 + (tb % 2)], in1=gt2, op0=ALU.mult, op1=ALU.mult)
                st = tok("store%d_%d" % (seq, tb))
                store_t.append(st)
                I("sp", "dma_start", obt, [st], out=out_d[r0:r0 + 128, :], in_=ob, dma=True, semkey="d_store%d" % (tb % 2))

        for seq in range(NSEQ):
            in_proj(0, NormStream(seq, True, g_mix))
            forget_tables()
            fox_attention()
            in_proj(1)
            sb_attention()
            gates_merge()
            out_proj(seq)
            ple_prep(seq)
            mlp(NormStream(seq, False, g_mlp))
            ple_final(seq, NormStream(seq, False, g_ple))
        P.add("sp", None, reads=store_t)

        counters = P.assign()
        semh = {k: es.enter_context(nc.semaphore("s_" + k)) for k in counters}
        with nc.Block() as block:
            @block.tensor
            def _(e):
                P.emit_engine("pe", e, semh)

            @block.scalar
            def _(e):
                P.emit_engine("act", e, semh)

            @block.vector
            def _(e):
                P.emit_engine("dve", e, semh)

            @block.gpsimd
            def _(e):
                P.emit_engine("pool", e, semh)

            @block.sync
            def _(e):
                P.emit_engine("sp", e, semh)
    return nc


def kernel(x, p, g_mix, w_in, b_forget, b_gate, w_branch_fox, w_branch_sb, w_out,
           g_mlp, w_up, w_down, g_ple, w_ple_gate, w_ple, g_final):
    f = lambda a: np.ascontiguousarray(np.asarray(a, dtype=np.float32))
    x = f(x)
    p = f(p)
    common = {
        "w_in": f(w_in)[0], "b_forget": f(b_forget).reshape(1, 8), "b_gate": f(b_gate)[0],
        "w_bf": f(w_branch_fox)[0], "w_bs": f(w_branch_sb)[0], "w_out": f(w_out)[0],
        "g_mix": f(g_mix).reshape(1, DM), "g_mlp": f(g_mlp).reshape(1, DM), "g_ple": f(g_ple).reshape(1, DM),
        "g_final": f(g_final).reshape(1, DM), "w_up": f(w_up)[0], "w_down": f(w_down)[0],
        "w_pg": f(w_ple_gate)[0], "w_ple": f(w_ple)[0],
    }
    in_maps = []
    for c in range(NCORES):
        m = dict(common)
        m["x"] = x[c * NSEQ:(c + 1) * NSEQ].reshape(NSEQ * S, DM)
        m["p"] = p[0, c * NSEQ:(c + 1) * NSEQ].reshape(NSEQ * S, 256)
        in_maps.append(m)
    nc = build_program()
    res = run_bass_kernel_spmd(nc, in_maps, core_ids=list(range(NCORES)))
    out = np.stack([np.asarray(r["out"]).reshape(NSEQ, S, DM) for r in res.results], axis=0)
    return out.reshape(NCORES * NSEQ, S, DM).astype(np.float32)
```
